# Optimizing a Trainium2 kernel written in Bass

```python
import jax, jax.numpy as jnp
from jax import lax
import numpy as np

D_MODEL = 4096
BATCH = 2
SEQ = 8192
DEPTH = 1

SB_HEADS = 16
SB_HEAD_DIM = 128
SB_WIDTH = SB_HEADS * SB_HEAD_DIM
SB_BLOCK = 128
CONV_WIDTH = D_MODEL // 2
CONV_KERNEL = 31
N_BRANCHES = 2
SPLIT_Q = SB_WIDTH
SPLIT_K = 2 * SB_WIDTH
SPLIT_V = 3 * SB_WIDTH
SPLIT_GLU = 3 * SB_WIDTH + 2 * CONV_WIDTH
IN_COLS = SPLIT_GLU + N_BRANCHES * D_MODEL
PEER_HEADS = 8
PEER_KEYS = 128
PEER_EXPERTS = PEER_KEYS * PEER_KEYS
PEER_QDIM = 256
PEER_HALF = PEER_QDIM // 2
PEER_TOPK = 16
PEER_CHUNK = 128
ALPHA = (2 * DEPTH) ** 0.25
BETA = (8 * DEPTH) ** -0.25
LN_EPS = 1e-5
N_MOD = 6

kernel_name = 'hybrid_sbattn_conformer_peer_deepnorm_adaln'


def _layer_norm(x, g=None, b=None):
    xf = x.astype(jnp.float32)
    mu = jnp.mean(xf, axis=-1, keepdims=True)
    var = jnp.mean(jnp.square(xf - mu), axis=-1, keepdims=True)
    y = (xf - mu) * lax.rsqrt(var + LN_EPS)
    if g is not None:
        y = y * g.astype(jnp.float32) + b.astype(jnp.float32)
    return y.astype(x.dtype)


def _split_heads(t):
    b, s, _ = t.shape
    return t.reshape(b, s, SB_HEADS, SB_HEAD_DIM).transpose(0, 2, 1, 3)


def _stick_breaking_attention(q, k, v):
    seq = q.shape[2]
    scale = SB_HEAD_DIM ** -0.5
    outs = []
    for start in range(0, seq, SB_BLOCK):
        end = start + SB_BLOCK
        qb = q[:, :, start:end].astype(jnp.float32)
        kb = k[:, :, :end].astype(jnp.float32)
        z = jnp.einsum('bhqd,bhkd->bhqk', qb, kb) * scale
        t_pos = jnp.arange(start, end)[:, None]
        s_pos = jnp.arange(end)[None, :]
        causal = s_pos < t_pos
        log_beta = jax.nn.log_sigmoid(z)
        log_keep = jnp.where(causal, log_beta - z, 0.0)
        suffix = lax.cumsum(log_keep, axis=3, reverse=True) - log_keep
        a = jnp.where(causal, jnp.exp(log_beta + suffix), 0.0)
        outs.append(jnp.einsum('bhqk,bhkd->bhqd', a.astype(v.dtype), v[:, :, :end]))
    return jnp.concatenate(outs, axis=2)


def _conv_module(u_glu, conv_w, conv_b, ln_g, ln_b):
    a, g = jnp.split(u_glu, 2, axis=-1)
    u = a * jax.nn.sigmoid(g)
    y = lax.conv_general_dilated(
        u, conv_w[:, None, :], window_strides=(1,),
        padding=[(CONV_KERNEL - 1, 0)],
        dimension_numbers=('NWC', 'WIO', 'NWC'),
        feature_group_count=CONV_WIDTH) + conv_b
    return jax.nn.silu(_layer_norm(y, ln_g, ln_b))


def _peer(h, wq, k1, k2, u_tab, v_tab):
    b, s, d = h.shape
    n_tok = b * s
    ht = h.reshape(n_tok, d)
    q = (ht @ wq).reshape(n_tok, PEER_HEADS, PEER_QDIM).astype(jnp.float32)
    q1, q2 = q[..., :PEER_HALF], q[..., PEER_HALF:]
    s1 = jnp.einsum('thd,hnd->thn', q1, k1.astype(jnp.float32))
    s2 = jnp.einsum('thd,hnd->thn', q2, k2.astype(jnp.float32))
    v1, i1 = lax.top_k(s1, PEER_TOPK)
    v2, i2 = lax.top_k(s2, PEER_TOPK)
    cand = (v1[..., :, None] + v2[..., None, :]).reshape(n_tok, PEER_HEADS, PEER_TOPK * PEER_TOPK)
    sc, flat = lax.top_k(cand, PEER_TOPK)
    e1 = jnp.take_along_axis(i1, flat // PEER_TOPK, axis=-1)
    e2 = jnp.take_along_axis(i2, flat % PEER_TOPK, axis=-1)
    idx = e1 * PEER_KEYS + e2
    gate = jax.nn.softmax(sc, axis=-1).astype(h.dtype)
    n_chunks = n_tok // PEER_CHUNK

    def chunk(args):
        hc, ic, gc = args
        ug = jnp.take(u_tab, ic, axis=0)
        act = jax.nn.gelu(jnp.einsum('cd,chkd->chk', hc, ug)) * gc
        vg = jnp.take(v_tab, ic, axis=0)
        return jnp.einsum('chk,chkd->cd', act, vg)

    out = lax.map(chunk, (
        ht.reshape(n_chunks, PEER_CHUNK, d),
        idx.reshape(n_chunks, PEER_CHUNK, PEER_HEADS, PEER_TOPK),
        gate.reshape(n_chunks, PEER_CHUNK, PEER_HEADS, PEER_TOPK)))
    return out.reshape(b, s, d)


def setup_inputs(seed: int = 0) -> dict:
    key = jax.random.key(seed)
    ks = jax.random.split(key, 32)
    f32 = jnp.float32
    L = DEPTH
    D = D_MODEL

    def nrm(k, shape, scale):
        return jax.random.normal(k, shape, f32) * scale

    x = nrm(ks[0], (BATCH, SEQ, D), 1.0)
    c = nrm(ks[1], (BATCH, D), 1.0)
    w_ada = nrm(ks[2], (L, D, N_MOD * D), 0.5 * D ** -0.5)
    b_ada = nrm(ks[3], (L, N_MOD * D), 0.01)
    w_qk = nrm(ks[4], (L, D, 2 * SB_WIDTH), D ** -0.5)
    w_v = nrm(ks[5], (L, D, SB_WIDTH), BETA * D ** -0.5)
    w_glu = nrm(ks[6], (L, D, 2 * CONV_WIDTH), D ** -0.5)
    w_gate = nrm(ks[7], (L, D, N_BRANCHES * D), D ** -0.5)
    w_in = jnp.concatenate([w_qk, w_v, w_glu, w_gate], axis=-1)
    conv_w = nrm(ks[8], (L, CONV_KERNEL, CONV_WIDTH), CONV_KERNEL ** -0.5)
    conv_b = nrm(ks[9], (L, CONV_WIDTH), 0.01)
    conv_ln_g = 1.0 + nrm(ks[10], (L, CONV_WIDTH), 0.05)
    conv_ln_b = nrm(ks[11], (L, CONV_WIDTH), 0.01)
    w_a_proj = nrm(ks[12], (L, SB_WIDTH, D), SB_WIDTH ** -0.5)
    w_b_proj = nrm(ks[13], (L, CONV_WIDTH, D), CONV_WIDTH ** -0.5)
    w_o = nrm(ks[14], (L, D, D), BETA * D ** -0.5)
    ln1_g = 1.0 + nrm(ks[15], (L, D), 0.05)
    ln1_b = nrm(ks[16], (L, D), 0.01)
    peer_wq = nrm(ks[17], (L, D, PEER_HEADS * PEER_QDIM), D ** -0.5)
    peer_k1 = nrm(ks[18], (L, PEER_HEADS, PEER_KEYS, PEER_HALF), PEER_HALF ** -0.5)
    peer_k2 = nrm(ks[19], (L, PEER_HEADS, PEER_KEYS, PEER_HALF), PEER_HALF ** -0.5)
    peer_u = nrm(ks[20], (L, PEER_EXPERTS, D), BETA * D ** -0.5)
    peer_v = nrm(ks[21], (L, PEER_EXPERTS, D), BETA)
    ln2_g = 1.0 + nrm(ks[22], (L, D), 0.05)
    ln2_b = nrm(ks[23], (L, D), 0.01)
    return {'x': x, 'c': c, 'w_ada': w_ada, 'b_ada': b_ada, 'w_in': w_in,
            'conv_w': conv_w, 'conv_b': conv_b, 'conv_ln_g': conv_ln_g, 'conv_ln_b': conv_ln_b,
            'w_a_proj': w_a_proj, 'w_b_proj': w_b_proj, 'w_o': w_o,
            'ln1_g': ln1_g, 'ln1_b': ln1_b,
            'peer_wq': peer_wq, 'peer_k1': peer_k1, 'peer_k2': peer_k2,
            'peer_u': peer_u, 'peer_v': peer_v, 'ln2_g': ln2_g, 'ln2_b': ln2_b}


def reference(x, c, w_ada, b_ada, w_in, conv_w, conv_b, conv_ln_g, conv_ln_b,
              w_a_proj, w_b_proj, w_o, ln1_g, ln1_b,
              peer_wq, peer_k1, peer_k2, peer_u, peer_v, ln2_g, ln2_b):
    b, s, _ = x.shape
    for l in range(DEPTH):
        mod = jnp.einsum('bd,de->be', jax.nn.silu(c), w_ada[l]) + b_ada[l]
        sh_m, sc_m, g_m, sh_f, sc_f, g_f = [m[:, None, :] for m in jnp.split(mod, N_MOD, axis=-1)]

        h = _layer_norm(x) * (1.0 + sc_m) + sh_m
        proj = h @ w_in[l]
        q, k, v, u_glu, gates = jnp.split(proj, [SPLIT_Q, SPLIT_K, SPLIT_V, SPLIT_GLU], axis=-1)
        attn = _stick_breaking_attention(_split_heads(q), _split_heads(k), _split_heads(v))
        y_a = attn.transpose(0, 2, 1, 3).reshape(b, s, SB_WIDTH) @ w_a_proj[l]
        y_b = _conv_module(u_glu, conv_w[l], conv_b[l], conv_ln_g[l], conv_ln_b[l]) @ w_b_proj[l]
        g_a, g_b = jnp.split(jax.nn.sigmoid(gates), N_BRANCHES, axis=-1)
        mixed = (g_a * y_a + g_b * y_b) @ w_o[l]
        x = _layer_norm(ALPHA * x + g_m * mixed, ln1_g[l], ln1_b[l])

        h = _layer_norm(x) * (1.0 + sc_f) + sh_f
        y_f = _peer(h, peer_wq[l], peer_k1[l], peer_k2[l], peer_u[l], peer_v[l])
        x = _layer_norm(ALPHA * x + g_f * y_f, ln2_g[l], ln2_b[l])
    return x
```

```python
import numpy as np
import concourse.bass as bass
import concourse.mybir as mybir
from concourse.bass_utils import run_bass_kernel_spmd

F32 = mybir.dt.float32
BF16 = mybir.dt.bfloat16
AF = mybir.ActivationFunctionType
ALU = mybir.AluOpType
AX = mybir.AxisListType

NEG_BIG = -1.0e5


class Cfg:
    def __init__(s, D=4096, S=8192, H=16, CW=2048, PH=8, depth=1):
        s.D, s.S, s.H, s.CW, s.PH = D, S, H, CW, PH
        s.KC = D // 128
        s.SBW = H * 128
        s.CPB = 4
        s.TO = S // s.CPB
        s.NK = 128
        s.NE = 128 * 128
        s.PQ = 256
        s.CK = 31
        s.IN_COLS = 3 * s.SBW + 2 * CW + 2 * D
        s.ALPHA = float((2 * depth) ** 0.25)
        s.EPS = 1e-5
        s.QO, s.KO, s.VO = 0, s.SBW, 2 * s.SBW
        s.GAO = 3 * s.SBW
        s.GGO = 3 * s.SBW + CW
        s.TAO = 3 * s.SBW + 2 * CW
        s.TBO = s.TAO + D


class Buf:
    __slots__ = ("name", "w", "r")

    def __init__(s, name=""):
        s.name = name
        s.w = {}
        s.r = {}


class Op:
    __slots__ = ("eng", "fn", "deps", "signal", "cnt", "dsem", "dcnt")


class DSem:
    def __init__(s, h):
        s.h = h
        s.count = 0


class Prog:
    ENGS = ("pe", "act", "dve", "pool", "sp")

    def __init__(s, nc):
        s.nc = nc
        s.ops = {e: [] for e in s.ENGS}
        s.esem = {e: nc.alloc_semaphore("es_" + e) for e in s.ENGS}
        s.dsems = []
        s.free_d = []
        s.phase_d = []
        s.pending = {e: [] for e in s.ENGS}

    def dma_sem(s, persistent=False):
        if s.free_d and not persistent:
            d = s.free_d.pop()
        else:
            d = DSem(s.nc.alloc_semaphore("ds%d" % len(s.dsems)))
            s.dsems.append(d)
        if not persistent:
            s.phase_d.append(d)
        return d

    def _collect(s, eng, reads, writes):
        deps = []
        for b in reads:
            for k, t in b.w.items():
                deps.append((t, "raw"))
        for b in writes:
            for k, t in b.w.items():
                deps.append((t, "waw"))
            for k, t in b.r.items():
                deps.append((t, "war"))
        if s.pending[eng]:
            deps += [(t, "raw") for t in s.pending[eng]]
            s.pending[eng] = []
        return deps

    def op(s, eng, fn, reads=(), writes=()):
        o = Op()
        o.eng, o.fn, o.signal, o.cnt, o.dsem, o.dcnt = eng, fn, False, 0, None, 0
        o.deps = s._collect(eng, reads, writes)
        tok = ("op", o)
        for b in reads:
            b.r[eng] = tok
        for b in writes:
            b.w = {eng: tok}
            b.r = {}
        s.ops[eng].append(o)
        return o

    def dma(s, eng, out, in_, sem, reads=(), writes=(), **kw):
        o = Op()
        o.eng, o.signal, o.cnt = eng, False, 0
        o.fn = lambda e: e.dma_start(out=out, in_=in_, **kw)
        o.deps = s._collect(eng, reads, writes)
        sem.count += 16
        o.dsem, o.dcnt = sem, sem.count
        tok = ("dma", sem, sem.count)
        key = ("d", id(sem))
        for b in reads:
            b.r[key] = tok
        for b in writes:
            b.w = {key: tok}
            b.r = {}
        s.ops[eng].append(o)
        return tok

    def barrier(s):
        toks = []
        for e in s.ENGS:
            if e == "sp":
                continue
            for o in reversed(s.ops[e]):
                if o.dsem is None:
                    toks.append(("op", o))
                    break
        for d in s.dsems:
            if d.count:
                toks.append(("dma", d, d.count))
        for e in s.ENGS:
            s.pending[e] = list(toks)
        s.free_d += s.phase_d
        s.phase_d = []

    def emit(s):
        nc = s.nc
        for e in s.ENGS:
            for o in s.ops[e]:
                for t, kind in o.deps:
                    if t[0] == "op":
                        p = t[1]
                        if p.eng != e:
                            p.signal = True
                        elif kind == "raw" and e != "pe":
                            p.signal = True
        for e in s.ENGS:
            c = 0
            for o in s.ops[e]:
                if o.signal:
                    c += 1
                    o.cnt = c
        engobj = {"pe": "tensor", "act": "scalar", "dve": "vector", "pool": "gpsimd", "sp": "sync"}

        def run(e, eng):
            seen = {}
            for o in s.ops[e]:
                need = {}
                for t, kind in o.deps:
                    if t[0] == "op":
                        p = t[1]
                        if p.eng == e and not (kind == "raw" and e != "pe"):
                            continue
                        if not p.signal:
                            continue
                        sh, c = s.esem[p.eng], p.cnt
                    else:
                        sh, c = t[1].h, t[2]
                    k = id(sh)
                    if seen.get(k, 0) >= c:
                        continue
                    if k not in need or need[k][1] < c:
                        need[k] = (sh, c)
                for k, (sh, c) in need.items():
                    eng.wait_ge(sh, c)
                    seen[k] = c
                ins = o.fn(eng)
                if o.dsem is not None:
                    ins.then_inc(o.dsem.h, 16)
                elif o.signal:
                    ins.then_inc(s.esem[e], 1)
            if e == "sp":
                for d in s.dsems:
                    if d.count and seen.get(id(d.h), 0) < d.count:
                        eng.wait_ge(d.h, d.count)

        with nc.Block() as block:
            @block.tensor
            def _(eng):
                run("pe", eng)

            @block.scalar
            def _(eng):
                run("act", eng)

            @block.vector
            def _(eng):
                run("dve", eng)

            @block.gpsimd
            def _(eng):
                run("pool", eng)

            @block.sync
            def _(eng):
                run("sp", eng)


class Arena:
    def __init__(s, nc, base, limit):
        s.nc, s.base, s.off, s.limit, s.n = nc, base, base, limit, 0

    def reset(s):
        s.off = s.base

    def alloc(s, name, shape, dtype):
        esz = 2 if dtype == BF16 else 4
        per = 1
        for d in shape[1:]:
            per *= d
        nbytes = (per * esz + 63) // 64 * 64
        assert s.off + nbytes <= s.limit, ("SBUF overflow", name, s.off, nbytes, s.limit)
        s.n += 1
        t = s.nc.alloc_sbuf_tensor_at("%s_%d" % (name, s.n), list(shape), dtype, offset=s.off)
        s.off += nbytes
        return t


class Ring:
    def __init__(s, P, arena, name, shape, dtype, n, dma=False):
        s.t = [arena.alloc(name, shape, dtype) for _ in range(n)]
        s.b = [Buf(name) for _ in range(n)]
        s.sem = [P.dma_sem() for _ in range(n)] if dma else [None] * n
        s.i, s.n = 0, n

    def next(s):
        k = s.i % s.n
        s.i += 1
        return s.t[k], s.b[k], s.sem[k]


class _Stop(Exception):
    pass


def build(cfg, debug_outs=(), stop=None):
    nc = bass.Bass("TRN2", target_bir_lowering=False)
    P = Prog(nc)
    try:
        _build_body(nc, P, cfg, debug_outs, stop)
    except _Stop:
        pass
    P.emit()
    return nc


def _build_body(nc, P, cfg, debug_outs, stop):
    phase = [0]

    def end_phase():
        P.barrier()
        phase[0] += 1
        if stop is not None and phase[0] >= stop:
            raise _Stop()

    D, S, H, CW, PH, KC, TO, NE = cfg.D, cfg.S, cfg.H, cfg.CW, cfg.PH, cfg.KC, cfg.TO, cfg.NE
    SBW = cfg.SBW
    CB = CW // 128
    NST = TO // 512
    TE = TO + 128

    def din(name, shape, dt=F32):
        return nc.dram_tensor(name, list(shape), dt, kind="ExternalInput").ap()

    def dscr(name, shape, dt):
        kind = "ExternalOutput" if name in debug_outs else "Internal"
        return nc.dram_tensor(name, list(shape), dt, kind=kind).ap()

    xb = din("xb", [S, D])
    xe = din("xe", [TE, D])
    cvec = din("cvec", [KC, 128])
    meta = din("meta", [128, 4])
    w_ada = din("w_ada", [D, 6 * D])
    b_ada = din("b_ada", [6 * KC, 128])
    w_in = din("w_in", [D, cfg.IN_COLS])
    conv_w = din("conv_w", [cfg.CK, CW])
    conv_b = din("conv_b", [CB, 128])
    conv_g = din("conv_ln_g", [CB, 128])
    conv_bb = din("conv_ln_b", [CB, 128])
    w_a = din("w_a_proj", [SBW, D])
    w_b = din("w_b_proj", [CW, D])
    w_o = din("w_o", [D, D])
    ln1_g = din("ln1_g", [1, D])
    ln1_b = din("ln1_b", [1, D])
    wq = din("peer_wq", [D, PH * 256])
    pk1 = din("peer_k1", [PH, 128, 128])
    pk2 = din("peer_k2", [PH, 128, 128])
    pu = din("peer_u", [NE, D])
    pv = din("peer_v", [NE, D])
    ln2_g = din("ln2_g", [1, D])
    ln2_b = din("ln2_b", [1, D])
    yout = nc.dram_tensor("y", [TO, D], F32, kind="ExternalOutput").ap()

    win_bf = dscr("win_bf", [D, cfg.IN_COLS], BF16)
    wa_bf = dscr("wa_bf", [SBW, D], BF16)
    wb_bf = dscr("wb_bf", [CW, D], BF16)
    wo_bf = dscr("wo_bf", [D, D], BF16)
    wq_bf = dscr("wq_bf", [D, PH * 256], BF16)
    pv_bf = dscr("pv_bf", [NE, D], BF16)
    puT_bf = dscr("puT_bf", [D, NE], BF16)
    hTb = dscr("hTb", [KC, 128, S], BF16)
    hTe = dscr("hTe", [KC, 128, TE], BF16)
    kTs = dscr("kTs", [H, 128, S], BF16)
    vvs = dscr("vvs", [S, SBW], BF16)
    qTs = dscr("qTs", [H, 128, TE], BF16)
    uTs = dscr("uTs", [CB, 128, TE], F32)
    cvTs = dscr("cvTs", [CB, 128, TO], BF16)
    atTs = dscr("atTs", [H, 128, TO], BF16)
    sgTs = dscr("sgTs", [2 * KC, 128, TE], BF16)
    yaTs = dscr("yaTs", [KC, 128, TO], F32)
    mxTs = dscr("mxTs", [KC, 128, TO], BF16)
    pre1 = dscr("pre1", [TO, D], F32)
    x1s = dscr("x1s", [TO, D], F32)
    h2Ts = dscr("h2Ts", [KC, 128, TO], BF16)
    pqTs = dscr("pqTs", [2 * PH, 128, TO], F32)
    GTs = dscr("GTs", [128, 128, TO], BF16)
    acTs = dscr("acTs", [128, 128, TO], BF16)
    yfp = dscr("yfp", [4, TO, D], F32)
    pre2 = dscr("pre2", [TO, D], F32)
    gvec = dscr("gvec", [2, D], F32)
    modrow_d = dscr("modrow", [1, 6 * D], F32)

    dbuf = {}
    for nm in ["win_bf", "wa_bf", "wb_bf", "wo_bf", "wq_bf", "pv_bf", "puT_bf", "hTb", "hTe", "kTs", "vvs", "qTs",
               "uTs", "cvTs", "atTs", "sgTs", "yaTs", "mxTs", "pre1", "x1s", "h2Ts", "pqTs", "GTs", "acTs", "yfp",
               "pre2", "gvec", "yout", "modrow"]:
        dbuf[nm] = Buf(nm)

    SB_BASE = 16640
    SB_LIMIT = 229344 - 64
    pers = Arena(nc, SB_BASE, SB_BASE + 28 * 1024)
    ident_f = pers.alloc("ident_f", [128, 128], F32)
    ident_b = pers.alloc("ident_b", [128, 128], BF16)
    ones_f = pers.alloc("ones_f", [128, 128], F32)
    ntri_b = pers.alloc("ntri_b", [128, 128], BF16)
    ntrs_b = pers.alloc("ntrs_b", [128, 128], BF16)
    tmp_f = pers.alloc("tmp_f", [128, 128], F32)
    metat = pers.alloc("metat", [128, 4], F32)
    modT = pers.alloc("modT", [128, 6 * KC], F32)
    cbT = pers.alloc("cbT", [128, CB], F32)
    cgT = pers.alloc("cgT", [128, CB], F32)
    cbbT = pers.alloc("cbbT", [128, CB], F32)
    cwT = pers.alloc("cwT", [128, CB, cfg.CK], F32)
    k1T = pers.alloc("k1T", [128, 2 * PH, 128], F32)
    B_const = Buf("const")
    B_mod = Buf("modT")
    A = Arena(nc, pers.off, SB_LIMIT)

    psum = [nc.alloc_psum_tensor("ps%d" % i, [128, 512], F32) for i in range(8)]
    psb = [Buf("ps%d" % i) for i in range(8)]

    S0 = P.dma_sem(persistent=True)

    def act_copy_alt(i):
        return "act" if i % 2 == 0 else "dve"

    def copy_op(eng, out, in_, reads, writes):
        if eng == "act":
            return P.op("act", lambda e: e.activation(out=out, in_=in_, func=AF.Copy), reads, writes)
        if eng == "dve":
            return P.op("dve", lambda e: e.tensor_copy(out=out, in_=in_), reads, writes)
        return P.op("pool", lambda e: e.tensor_copy(out=out, in_=in_), reads, writes)

    P.op("pool", lambda e: e.memset(ones_f[:], 1.0), writes=[B_const])
    P.op("pool", lambda e: e.affine_select(out=ident_f[:], in_=ones_f[:], pattern=[[-1, 128]], compare_op=ALU.is_equal,
                                           fill=0.0, base=0, channel_multiplier=1), reads=[B_const], writes=[B_const])
    P.op("pool", lambda e: e.tensor_copy(out=ident_b[:], in_=ident_f[:]), reads=[B_const], writes=[B_const])
    P.op("pool", lambda e: e.memset(tmp_f[:], -1.0), writes=[B_const])
    P.op("pool", lambda e: e.affine_select(out=ntri_b[:], in_=tmp_f[:], pattern=[[-1, 128]], compare_op=ALU.is_ge,
                                           fill=0.0, base=0, channel_multiplier=1), reads=[B_const], writes=[B_const])
    P.op("pool", lambda e: e.affine_select(out=ntrs_b[:], in_=tmp_f[:], pattern=[[1, 128]], compare_op=ALU.is_gt,
                                           fill=0.0, base=0, channel_multiplier=-1), reads=[B_const], writes=[B_const])
    P.dma("sp", metat[:], meta, S0, writes=[B_const])

    def convert(src, dst, rows, cols, key):
        sem = P.dma_sem(persistent=True)
        rstep = max(1, (2 * 1024 * 1024) // cols)
        r = 0
        while r < rows:
            r1 = min(rows, r + rstep)
            P.dma("pool", dst[r:r1, :], src[r:r1, :], sem, writes=[dbuf[key]], max_dma_last_dim=4096)
            r = r1

    convert(w_in, win_bf, D, cfg.IN_COLS, "win_bf")

    def load_fm_table(src2d, nrows, dst_tab, dst_buf, col0=0):
        r = 0
        while r < nrows:
            n = min(128, nrows - r)
            st, sb, ss = misc_ld.next()
            P.dma("sp", st[0:n, :], src2d[r:r + n, :], ss, writes=[sb])
            P.op("pe", lambda e, st=st, n=n: e.transpose(out=psum[7][:, 0:n], in_=st[0:n, :], identity=ident_f[0:n, 0:n]),
                 reads=[sb, B_const], writes=[psb[7]])
            P.op("dve", lambda e, r=r, n=n: e.tensor_copy(out=dst_tab[:, col0 + r:col0 + r + n], in_=psum[7][:, 0:n]),
                 reads=[psb[7]], writes=[dst_buf])
            r += n

    A.reset()
    misc_ld = Ring(P, A, "miscld", [128, 128], F32, 2, dma=True)
    cT = A.alloc("cT", [128, KC], F32)
    bT = A.alloc("bT", [128, 6 * KC], F32)
    B_cT, B_bT = Buf("cT"), Buf("bT")
    st, sb, ss = misc_ld.next()
    P.dma("sp", st[0:KC, :], cvec, ss, writes=[sb])
    P.op("act", lambda e, st=st: e.activation(out=st[0:KC, :], in_=st[0:KC, :], func=AF.Silu), reads=[sb], writes=[sb])
    P.op("pe", lambda e, st=st: e.transpose(out=psum[7][:, 0:KC], in_=st[0:KC, :], identity=ident_f[0:KC, 0:KC]),
         reads=[sb, B_const], writes=[psb[7]])
    P.op("dve", lambda e: e.tensor_copy(out=cT[:], in_=psum[7][:, 0:KC]), reads=[psb[7]], writes=[B_cT])
    load_fm_table(b_ada, 6 * KC, bT, B_bT)
    load_fm_table(conv_b, CB, cbT, B_const)
    load_fm_table(conv_g, CB, cgT, B_const)
    load_fm_table(conv_bb, CB, cbbT, B_const)
    cwst = A.alloc("cwst", [cfg.CK, CW], F32)
    B_cwst = Buf("cwst")
    P.dma("sp", cwst[:], conv_w, P.dma_sem(), writes=[B_cwst])
    for cb in range(CB):
        P.op("pe", lambda e, cb=cb: e.transpose(out=psum[7][:, 0:cfg.CK], in_=cwst[:, cb * 128:(cb + 1) * 128],
                                                identity=ident_f[0:cfg.CK, 0:cfg.CK]),
             reads=[B_cwst, B_const], writes=[psb[7]])
        P.op("dve", lambda e, cb=cb: e.tensor_copy(out=cwT[:, cb, :], in_=psum[7][:, 0:cfg.CK]),
             reads=[psb[7]], writes=[B_const])
    for hh in range(2 * PH):
        src = (pk1 if hh % 2 == 0 else pk2)[hh // 2]
        st, sb, ss = misc_ld.next()
        P.dma("sp", st[:], src, ss, writes=[sb])
        P.op("pe", lambda e, st=st: e.transpose(out=psum[7][:, 0:128], in_=st[:], identity=ident_f[:]),
             reads=[sb, B_const], writes=[psb[7]])
        P.op("dve", lambda e, hh=hh: e.tensor_copy(out=k1T[:, hh, :], in_=psum[7][:, 0:128]),
             reads=[psb[7]], writes=[B_const])

    CG = 256
    wad = Ring(P, A, "wad", [128, KC, CG], F32, 2, dma=True)
    rst = Ring(P, A, "rowst", [1, CG], F32, 4, dma=True)
    modraw = A.alloc("modraw", [128, 6 * KC], F32)
    B_mraw = Buf("modraw")
    w_ada_v = w_ada.rearrange("(kc p) n -> p kc n", p=128)
    for g in range(6 * D // CG):
        wt, wbuf, wsem = wad.next()
        P.dma("sp", wt[:], w_ada_v[:, :, g * CG:(g + 1) * CG], wsem, writes=[wbuf])
        b = 4 + g % 2
        for kc in range(KC):
            P.op("pe", lambda e, wt=wt, kc=kc, b=b: e.matmul(
                psum[b][0:1, 0:CG], lhsT=cT[:, kc:kc + 1], rhs=wt[:, kc, :],
                start=(kc == 0), stop=(kc == KC - 1)), reads=[wbuf, B_cT], writes=[psb[b]])
        rs, rsb, rss = rst.next()
        copy_op(act_copy_alt(g), rs[0:1, :], psum[b][0:1, 0:CG], [psb[b]], [rsb])
        P.dma("act", modrow_d[0:1, g * CG:(g + 1) * CG], rs[0:1, :], rss, reads=[rsb], writes=[dbuf["modrow"]])
    end_phase()
    convert(w_a, wa_bf, SBW, D, "wa_bf")
    convert(w_b, wb_bf, CW, D, "wb_bf")
    convert(w_o, wo_bf, D, D, "wo_bf")
    convert(wq, wq_bf, D, PH * 256, "wq_bf")
    convert(pv, pv_bf, NE, D, "pv_bf")
    load_fm_table(modrow_d[0].rearrange("(r p) -> r p", p=128), 6 * KC, modraw, B_mraw)
    P.op("dve", lambda e: e.tensor_tensor(out=modT[:], in0=modraw[:], in1=bT[:], op=ALU.add),
         reads=[B_mraw, B_bT], writes=[B_mod])
    for m in (1, 4):
        P.op("dve", lambda e, m=m: e.tensor_scalar(out=modT[:, m * KC:(m + 1) * KC], in0=modT[:, m * KC:(m + 1) * KC],
                                                   scalar1=1.0, scalar2=None, op0=ALU.add), reads=[B_mod], writes=[B_mod])
    gst = A.alloc("gst", [KC, 2, 128], F32)
    B_gst = Buf("gst")
    for i, m in enumerate((2, 5)):
        P.op("pe", lambda e, m=m: e.transpose(out=psum[7][0:KC, 0:128], in_=modT[:, m * KC:(m + 1) * KC], identity=ident_f[:]),
             reads=[B_mod, B_const], writes=[psb[7]])
        P.op("dve", lambda e, i=i: e.tensor_copy(out=gst[:, i, :], in_=psum[7][0:KC, 0:128]), reads=[psb[7]], writes=[B_gst])
        P.dma("sp", gvec[i].rearrange("(kc p) -> kc p", p=128), gst[:, i, :], S0, reads=[B_gst], writes=[dbuf["gvec"]])
    end_phase()

    def ln_phase(src, ntok, mode, sc_m=None, sh_m=None, dstT=None, dstT_key=None, gb=None, dst=None, dst_key=None,
                 src_key=None):
        A.reset()
        xr = Ring(P, A, "lnx", [128, D], F32, 2, dma=True)
        xn_r = Ring(P, A, "lnxn", [128, D], F32, 2, dma=(mode == "affine"))
        nch = (D + 511) // 512
        stt = A.alloc("lnst", [128, nch, 6], F32)
        mv = A.alloc("lnmv", [128, 8], F32)
        B_st = Buf("lnst")
        if mode == "modT":
            hst = Ring(P, A, "hst", [128, KC, 512], BF16, 2, dma=True)
        else:
            g_bc = A.alloc("g_bc", [128, D], F32)
            b_bc = A.alloc("b_bc", [128, D], F32)
            B_gb = Buf("gb")
            P.dma("sp", g_bc[:], gb[0].partition_broadcast(128), S0, writes=[B_gb])
            P.dma("sp", b_bc[:], gb[1].partition_broadcast(128), S0, writes=[B_gb])
        t = 0
        while t < ntok:
            T = min(512, ntok - t)
            if mode == "modT":
                hs, hb, hsem = hst.next()
            for tt in range(T // 128):
                r0 = t + tt * 128
                xt, xbf, xs = xr.next()
                rd = [dbuf[src_key]] if src_key else []
                P.dma("sp", xt[:], src[r0:r0 + 128, :], xs, reads=rd, writes=[xbf])
                for c in range(nch):
                    c1 = min(D, (c + 1) * 512)
                    P.op("dve", lambda e, xt=xt, c=c, c1=c1: e.bn_stats(out=stt[:, c, :], in_=xt[:, c * 512:c1]),
                         reads=[xbf], writes=[B_st])
                P.op("dve", lambda e: e.bn_aggr(out=mv[:, 0:2], in_=stt[:].rearrange("p c s -> p (c s)")),
                     reads=[B_st], writes=[B_st])
                P.op("dve", lambda e: e.tensor_scalar(out=mv[:, 2:3], in0=mv[:, 1:2], scalar1=cfg.EPS, scalar2=None,
                                                      op0=ALU.add), reads=[B_st], writes=[B_st])
                P.op("act", lambda e: e.activation(out=mv[:, 3:4], in_=mv[:, 2:3], func=AF.Sqrt), reads=[B_st], writes=[B_st])
                P.op("dve", lambda e: e.reciprocal(out=mv[:, 4:5], in_=mv[:, 3:4]), reads=[B_st], writes=[B_st])
                P.op("dve", lambda e: e.tensor_scalar(out=mv[:, 5:6], in0=mv[:, 0:1], scalar1=mv[:, 4:5], scalar2=-1.0,
                                                      op0=ALU.mult, op1=ALU.mult), reads=[B_st], writes=[B_st])
                xn, xnb, xns = xn_r.next()
                P.op("act", lambda e, xn=xn, xt=xt: e.activation(out=xn[:], in_=xt[:], func=AF.Identity,
                                                                 bias=mv[:, 5:6], scale=mv[:, 4:5]),
                     reads=[xbf, B_st], writes=[xnb])
                if mode == "modT":
                    for k0 in range(0, KC, 4):
                        nk = min(4, KC - k0)
                        pb = 4 + (k0 // 4) % 2
                        for j in range(nk):
                            kc = k0 + j
                            P.op("pe", lambda e, xn=xn, kc=kc, j=j, pb=pb: e.transpose(
                                out=psum[pb][:, j * 128:(j + 1) * 128], in_=xn[:, kc * 128:(kc + 1) * 128],
                                identity=ident_f[:]), reads=[xnb, B_const], writes=[psb[pb]])
                        for j in range(nk):
                            kc = k0 + j
                            P.op("act", lambda e, hs=hs, kc=kc, j=j, pb=pb, tt=tt: e.activation(
                                out=hs[:, kc, tt * 128:(tt + 1) * 128], in_=psum[pb][:, j * 128:(j + 1) * 128],
                                func=AF.Identity, bias=modT[:, sh_m * KC + kc:sh_m * KC + kc + 1],
                                scale=modT[:, sc_m * KC + kc:sc_m * KC + kc + 1]),
                                 reads=[psb[pb], B_mod], writes=[hb])
                else:
                    P.op("dve", lambda e, xn=xn: e.tensor_tensor(out=xn[:], in0=xn[:], in1=g_bc[:], op=ALU.mult),
                         reads=[xnb, B_gb], writes=[xnb])
                    P.op("pool", lambda e, xn=xn: e.tensor_tensor(out=xn[:], in0=xn[:], in1=b_bc[:], op=ALU.add),
                         reads=[xnb, B_gb], writes=[xnb])
                    P.dma("act", dst[r0:r0 + 128, :], xn[:], xns, reads=[xnb], writes=[dbuf[dst_key]])
            if mode == "modT":
                P.dma("act", dstT.rearrange("kc p t -> p kc t")[:, :, t:t + T], hs[:, :, 0:T], hsem, reads=[hb],
                      writes=[dbuf[dstT_key]])
            t += T
        end_phase()

    def gemm(xT, xkey, kc0, Kc, tok_ranges, groups, mode, epi, wkey, banks=(0, 1, 2, 3, 4, 5, 6, 7), xbufs=3, wbufs=2,
             arena_reset=True, pre=None):
        if arena_reset:
            A.reset()
        xr = Ring(P, A, "gx", [128, Kc, 512], BF16, xbufs, dma=True)
        wr = Ring(P, A, "gw", [128, Kc, 512], BF16, wbufs, dma=True)
        if pre is not None:
            pre()
        xv = xT.rearrange("kc p t -> p kc t")
        bi = 0
        pairs = [tok_ranges[i:i + 2] for i in range(0, len(tok_ranges), 2)]
        for pr in pairs:
            xs_l = []
            for (t0, T) in pr:
                xt, xbf, xs = xr.next()
                P.dma("sp", xt[:, :, 0:T], xv[:, kc0:kc0 + Kc, t0:t0 + T], xs, reads=[dbuf[xkey]], writes=[xbf])
                xs_l.append((xt, xbf, t0, T))
            for gi, pieces in enumerate(groups):
                wt, wbf, ws = wr.next()
                c = 0
                for (wap, n) in pieces:
                    P.dma("sp", wt[:, :, c:c + n], wap, ws, reads=[dbuf[wkey]] if wkey else [], writes=[wbf])
                    c += n
                ncols = c
                for (xt, xbf, t0, T) in xs_l:
                    if mode == "fm":
                        pl = []
                        for nb in range(ncols // 128):
                            b = banks[bi % len(banks)]
                            bi += 1
                            for kc in range(Kc):
                                P.op("pe", lambda e, b=b, wt=wt, xt=xt, kc=kc, nb=nb, T=T: e.matmul(
                                    psum[b][:, 0:T], lhsT=wt[:, kc, nb * 128:(nb + 1) * 128], rhs=xt[:, kc, 0:T],
                                    start=(kc == 0), stop=(kc == Kc - 1)), reads=[wbf, xbf], writes=[psb[b]])
                            pl.append(b)
                        epi(gi, t0, T, pl)
                    else:
                        for tt in range(T // 128):
                            b = banks[bi % len(banks)]
                            bi += 1
                            for kc in range(Kc):
                                P.op("pe", lambda e, b=b, wt=wt, xt=xt, kc=kc, tt=tt, ncols=ncols: e.matmul(
                                    psum[b][:, 0:ncols], lhsT=xt[:, kc, tt * 128:(tt + 1) * 128], rhs=wt[:, kc, 0:ncols],
                                    start=(kc == 0), stop=(kc == Kc - 1)), reads=[wbf, xbf], writes=[psb[b]])
                            epi(gi, t0 + tt * 128, ncols, b)
        end_phase()

    def wgroups(wbf_ap, c0, c1, Kc, kc0=0, step=512):
        wv = wbf_ap.rearrange("(kc p) n -> p kc n", p=128)
        out = []
        c = c0
        while c < c1:
            n = min(step, c1 - c)
            out.append([(wv[:, kc0:kc0 + Kc, c:c + n], n)])
            c += n
        return out

    def ranges(t0, t1, step=512):
        out = []
        t = t0
        while t < t1:
            out.append((t, min(step, t1 - t)))
            t += step
        return out

    ln_phase(xb, S, "modT", sc_m=1, sh_m=0, dstT=hTb, dstT_key="hTb")

    st_holder = {}

    def make_stage(name, shape, dtype, n=3):
        st_holder[name] = Ring(P, A, name, shape, dtype, n, dma=True)

    def epi_fm_store(dst, dkey, row_of_block, func=None, tcol0=0):
        cnt = [0]

        def epi(gi, t0, T, pl):
            for nb, b in enumerate(pl):
                stg, sbf, ssem = st_holder["stg"].next()
                cnt[0] += 1
                if func is None:
                    copy_op(act_copy_alt(cnt[0]), stg[:, 0:T], psum[b][:, 0:T], [psb[b]], [sbf])
                else:
                    P.op("act", lambda e, stg=stg, b=b, T=T: e.activation(out=stg[:, 0:T], in_=psum[b][:, 0:T], func=func),
                         reads=[psb[b]], writes=[sbf])
                P.dma("act", dst[row_of_block(gi, nb), :, tcol0 + t0:tcol0 + t0 + T], stg[:, 0:T], ssem, reads=[sbf],
                      writes=[dbuf[dkey]])
        return epi

    gemm(hTb, "hTb", 0, KC, ranges(0, S), wgroups(win_bf, cfg.KO, cfg.KO + SBW, KC), "fm",
         epi_fm_store(kTs, "kTs", lambda gi, nb: gi * 4 + nb), "win_bf",
         pre=lambda: make_stage("stg", [128, 512], BF16, 4))

    def epi_v(gi, r0, ncols, b):
        stg, sbf, ssem = st_holder["stg"].next()
        copy_op(act_copy_alt(gi + r0 // 128), stg[:, 0:ncols], psum[b][:, 0:ncols], [psb[b]], [sbf])
        P.dma("act", vvs[r0:r0 + 128, gi * 512:gi * 512 + ncols], stg[:, 0:ncols], ssem, reads=[sbf], writes=[dbuf["vvs"]])

    gemm(hTb, "hTb", 0, KC, ranges(0, S), wgroups(win_bf, cfg.VO, cfg.VO + SBW, KC), "tm", epi_v, "win_bf",
         pre=lambda: make_stage("stg", [128, 512], BF16, 4))

    ln_phase(xe, TE, "modT", sc_m=1, sh_m=0, dstT=hTe, dstT_key="hTe")
    own_ranges = [(0, 128)] + ranges(128, TE)
    gemm(hTe, "hTe", 0, KC, own_ranges, wgroups(win_bf, cfg.QO, cfg.QO + SBW, KC), "fm",
         epi_fm_store(qTs, "qTs", lambda gi, nb: gi * 4 + nb), "win_bf",
         pre=lambda: make_stage("stg", [128, 512], BF16, 4))
    gemm(hTe, "hTe", 0, KC, own_ranges, wgroups(win_bf, cfg.TAO, cfg.TAO + 2 * D, KC), "fm",
         epi_fm_store(sgTs, "sgTs", lambda gi, nb: gi * 4 + nb, func=AF.Sigmoid), "win_bf",
         pre=lambda: make_stage("stg", [128, 512], BF16, 4))

    win_v = win_bf.rearrange("(kc p) n -> p kc n", p=128)
    glu_groups = []
    for cb2 in range(CB // 2):
        glu_groups.append([(win_v[:, :, cfg.GAO + cb2 * 256:cfg.GAO + (cb2 + 1) * 256], 256),
                           (win_v[:, :, cfg.GGO + cb2 * 256:cfg.GGO + (cb2 + 1) * 256], 256)])

    def pre_glu():
        make_stage("sig", [128, 512], F32, 2)
        make_stage("ust", [128, 512], F32, 3)

    def epi_glu(gi, t0, T, pl):
        for j in range(2):
            ba, bg = pl[j], pl[2 + j]
            sg, sgb, _ = st_holder["sig"].next()
            P.op("act", lambda e, sg=sg, bg=bg, T=T: e.activation(out=sg[:, 0:T], in_=psum[bg][:, 0:T], func=AF.Sigmoid),
                 reads=[psb[bg]], writes=[sgb])
            us, usb, ussem = st_holder["ust"].next()
            P.op("dve", lambda e, us=us, sg=sg, ba=ba, T=T: e.tensor_tensor(out=us[:, 0:T], in0=psum[ba][:, 0:T],
                                                                            in1=sg[:, 0:T], op=ALU.mult),
                 reads=[psb[ba], sgb], writes=[usb])
            if t0 == 0:
                P.op("dve", lambda e, us=us, T=T: e.tensor_scalar(out=us[:, 0:T], in0=us[:, 0:T], scalar1=metat[:, 1:2],
                                                                  scalar2=None, op0=ALU.mult),
                     reads=[usb, B_const], writes=[usb])
            P.dma("act", uTs[gi * 2 + j, :, t0:t0 + T], us[:, 0:T], ussem, reads=[usb], writes=[dbuf["uTs"]])

    gemm(hTe, "hTe", 0, KC, own_ranges, glu_groups, "fm", epi_glu, "win_bf", pre=pre_glu)

    A.reset()
    CKn = cfg.CK
    uin = Ring(P, A, "uin", [128, 512 + CKn - 1], F32, 3, dma=True)
    ycv = A.alloc("ycv", [128, CB, 512], F32)
    B_ycv = [Buf("ycv%d" % i) for i in range(CB)]
    ysq = Ring(P, A, "ysq", [128, 512], F32, 2)
    mean_t = A.alloc("mean_t", [128, 512], F32)
    rstd_t = A.alloc("rstd_t", [128, 512], F32)
    B_ms = Buf("ms")
    cvo = Ring(P, A, "cvo", [128, 512], BF16, 3, dma=True)
    tmpc = Ring(P, A, "tmpc", [128, 512], F32, 2)
    import os as _os
    for st_i in list(range(NST)) * int(_os.environ.get("CONVREP", "1")):
        t0 = st_i * 512
        for cb in range(CB):
            ut, ubf, usem = uin.next()
            P.dma("sp", ut[:], uTs[cb, :, 128 + t0 - (CKn - 1):128 + t0 + 512], usem, reads=[dbuf["uTs"]], writes=[ubf])
            P.op("dve", lambda e, ut=ut, cb=cb: e.tensor_scalar(out=ycv[:, cb, :], in0=ut[:, 0:512], scalar1=cwT[:, cb, 0:1],
                                                                scalar2=cbT[:, cb:cb + 1], op0=ALU.mult, op1=ALU.add),
                 reads=[ubf, B_const], writes=[B_ycv[cb]])
            for k in range(1, CKn):
                P.op("dve", lambda e, ut=ut, cb=cb, k=k: e.scalar_tensor_tensor(
                    out=ycv[:, cb, :], in0=ut[:, k:k + 512], scalar=cwT[:, cb, k:k + 1], in1=ycv[:, cb, :],
                    op0=ALU.mult, op1=ALU.add), reads=[ubf, B_const, B_ycv[cb]], writes=[B_ycv[cb]])
            yq, yqb, _ = ysq.next()
            P.op("act", lambda e, yq=yq, cb=cb: e.activation(out=yq[:], in_=ycv[:, cb, :], func=AF.Square),
                 reads=[B_ycv[cb]], writes=[yqb])
            P.op("pe", lambda e, cb=cb: e.matmul(psum[0][:], lhsT=ones_f[:], rhs=ycv[:, cb, :], start=(cb == 0),
                                                 stop=(cb == CB - 1)), reads=[B_ycv[cb], B_const], writes=[psb[0]])
            P.op("pe", lambda e, cb=cb, yq=yq: e.matmul(psum[1][:], lhsT=ones_f[:], rhs=yq[:], start=(cb == 0),
                                                        stop=(cb == CB - 1)), reads=[yqb, B_const], writes=[psb[1]])
        P.op("dve", lambda e: e.tensor_scalar(out=mean_t[:], in0=psum[0][:], scalar1=1.0 / CW, scalar2=None, op0=ALU.mult),
             reads=[psb[0]], writes=[B_ms])
        P.op("dve", lambda e: e.tensor_tensor(out=rstd_t[:], in0=mean_t[:], in1=mean_t[:], op=ALU.mult),
             reads=[B_ms], writes=[B_ms])
        P.op("dve", lambda e: e.scalar_tensor_tensor(out=rstd_t[:], in0=psum[1][:], scalar=1.0 / CW, in1=rstd_t[:],
                                                     op0=ALU.mult, op1=ALU.subtract), reads=[psb[1], B_ms], writes=[B_ms])
        P.op("dve", lambda e: e.tensor_scalar(out=rstd_t[:], in0=rstd_t[:], scalar1=cfg.EPS, scalar2=None, op0=ALU.add),
             reads=[B_ms], writes=[B_ms])
        P.op("act", lambda e: e.activation(out=rstd_t[:], in_=rstd_t[:], func=AF.Sqrt), reads=[B_ms], writes=[B_ms])
        P.op("dve", lambda e: e.reciprocal(out=rstd_t[:], in_=rstd_t[:]), reads=[B_ms], writes=[B_ms])
        if "dbgm" in debug_outs and st_i == 0:
            dbgm = nc.dram_tensor("dbgm", [2, 128, 512], F32, kind="ExternalOutput").ap()
            dbgy = nc.dram_tensor("dbgy", [CB, 128, 512], F32, kind="ExternalOutput").ap()
            P.dma("sp", dbgm[0], mean_t[:], S0, reads=[B_ms])
            P.dma("sp", dbgm[1], rstd_t[:], S0, reads=[B_ms])
            for cb in range(CB):
                P.dma("sp", dbgy[cb], ycv[:, cb, :], S0, reads=[B_ycv[cb]])
        for cb in range(CB):
            tc_, tcb, _ = tmpc.next()
            P.op("dve", lambda e, tc_=tc_, cb=cb: e.tensor_tensor(out=tc_[:], in0=ycv[:, cb, :], in1=mean_t[:], op=ALU.subtract),
                 reads=[B_ycv[cb], B_ms], writes=[tcb])
            P.op("pool", lambda e, tc_=tc_: e.tensor_tensor(out=tc_[:], in0=tc_[:], in1=rstd_t[:], op=ALU.mult),
                 reads=[tcb, B_ms], writes=[tcb])
            co, cob, cosem = cvo.next()
            P.op("act", lambda e, co=co, tc_=tc_, cb=cb: e.activation(out=co[:], in_=tc_[:], func=AF.Silu,
                                                                      bias=cbbT[:, cb:cb + 1], scale=cgT[:, cb:cb + 1]),
                 reads=[tcb, B_const], writes=[cob])
            P.dma("act", cvTs[cb, :, t0:t0 + 512], co[:], cosem, reads=[cob], writes=[dbuf["cvTs"]])
    end_phase()

    A.reset()
    NKB = S // 128
    NREL = NKB + TO // 128 - 1 + 3
    masktab = A.alloc("masktab", [128, NREL * 128], BF16)
    mtmp = A.alloc("mtmp", [128, NREL * 128], F32)
    B_mask = Buf("mask")
    P.op("pool", lambda e: e.iota(mtmp[:].rearrange("p (r t) -> p r t", t=128), pattern=[[-128, NREL], [-1, 128]],
                                  base=(NKB - 1) * 128, channel_multiplier=1, allow_small_or_imprecise_dtypes=True),
         writes=[B_mask])
    P.op("dve", lambda e: e.tensor_scalar(out=masktab[:], in0=mtmp[:], scalar1=metat[:, 0:1], scalar2=NEG_BIG,
                                          op0=ALU.is_ge, op1=ALU.mult), reads=[B_mask, B_const], writes=[B_mask])
    kth = Ring(P, A, "kth", [128, S], BF16, 2, dma=True)
    vth = Ring(P, A, "vth", [128, NKB, 128], BF16, 2, dma=True)
    qth = Ring(P, A, "qth", [128, TO], BF16, 2, dma=True)
    zm_r = Ring(P, A, "zm", [128, 512], F32, 5)
    e_r = Ring(P, A, "ee", [128, 512], F32, 2)
    L_r = Ring(P, A, "LL", [128, 512], BF16, 6)
    ar_r = Ring(P, A, "arg", [128, 512], F32, 3)
    a_r = Ring(P, A, "aa", [128, 512], BF16, 5)
    ao_r = Ring(P, A, "ao", [128, 512], BF16, 2, dma=True)
    scale = 128.0 ** -0.5
    NG = TO // 512
    for h in range(H):
        kt, kb_, ks = kth.next()
        vt, vb_, vs = vth.next()
        qt, qb_, qs = qth.next()
        P.dma("sp", kt[:], kTs[h], ks, reads=[dbuf["kTs"]], writes=[kb_])
        P.dma("sp", vt[:], vvs[:, h * 128:(h + 1) * 128].rearrange("(kb p) d -> p kb d", p=128), vs, reads=[dbuf["vvs"]],
              writes=[vb_])
        P.dma("sp", qt[:], qTs[h, :, 128:128 + TO], qs, reads=[dbuf["qTs"]], writes=[qb_])
        for gp in range(0, NG, 2):
            gl = [g for g in (gp, gp + 1) if g < NG]
            tiles = []
            for kb in range(NKB - 1, -1, -1):
                for gi_, g in enumerate(gl):
                    tiles.append((gi_, g, kb))
            st8 = {}

            def stageA(n):
                gi_, g, kb = tiles[n]
                bS = 6 + (n % 2) if False else (n % 2)
                P.op("pe", lambda e, kt=kt, qt=qt, kb=kb, g=g, bS=bS: e.matmul(
                    psum[bS][:], lhsT=kt[:, kb * 128:(kb + 1) * 128], rhs=qt[:, g * 512:(g + 1) * 512],
                    start=True, stop=True), reads=[kb_, qb_], writes=[psb[bS]])
                r0 = g * 4 - kb + (NKB - 1)
                zm, zmb, _ = zm_r.next()
                P.op("dve", lambda e, zm=zm, bS=bS, r0=r0: e.tensor_tensor(
                    out=zm[:], in0=psum[bS][:], in1=masktab[:, r0 * 128:(r0 + 4) * 128], op=ALU.add),
                     reads=[psb[bS], B_mask], writes=[zmb])
                ee, eb, _ = e_r.next()
                P.op("act", lambda e, ee=ee, zm=zm: e.activation(out=ee[:], in_=zm[:], func=AF.Exp, scale=scale),
                     reads=[zmb], writes=[eb])
                LL, Lb, _ = L_r.next()
                P.op("act", lambda e, LL=LL, ee=ee: e.activation(out=LL[:], in_=ee[:], func=AF.Ln, bias=1.0),
                     reads=[eb], writes=[Lb])
                st8[n] = dict(zm=zm, zmb=zmb, LL=LL, Lb=Lb)

            def stageB1(n):
                gi_, g, kb = tiles[n]
                d_ = st8[n]
                bC = 2 + gi_
                first = (kb == NKB - 1)
                LL, Lb, zm, zmb = d_["LL"], d_["Lb"], d_["zm"], d_["zmb"]
                P.op("pe", lambda e, LL=LL, bC=bC, first=first: e.matmul(
                    psum[bC][:], lhsT=ntri_b[:], rhs=LL[:], start=first, stop=False, skip_group_check=True),
                     reads=[Lb, B_const], writes=[psb[bC]])
                ar, arb, _ = ar_r.next()
                P.op("dve", lambda e, ar=ar, zm=zm, bC=bC: e.scalar_tensor_tensor(
                    out=ar[:], in0=zm[:], scalar=scale, in1=psum[bC][:], op0=ALU.mult, op1=ALU.add),
                     reads=[zmb, psb[bC]], writes=[arb])
                aa, ab, _ = a_r.next()
                P.op("act", lambda e, aa=aa, ar=ar: e.activation(out=aa[:], in_=ar[:], func=AF.Exp),
                     reads=[arb], writes=[ab])
                d_["aa"], d_["ab"] = aa, ab

            def stageB2(n):
                gi_, g, kb = tiles[n]
                d_ = st8[n]
                bC = 2 + gi_
                last = (kb == 0)
                LL, Lb = d_["LL"], d_["Lb"]
                P.op("pe", lambda e, LL=LL, bC=bC, last=last: e.matmul(
                    psum[bC][:], lhsT=ntrs_b[:], rhs=LL[:], start=False, stop=last, skip_group_check=True),
                     reads=[Lb, B_const], writes=[psb[bC]])

            def stageB3(n):
                gi_, g, kb = tiles[n]
                d_ = st8.pop(n)
                bO = 4 + gi_
                first, last = (kb == NKB - 1), (kb == 0)
                aa, ab = d_["aa"], d_["ab"]
                P.op("pe", lambda e, vt=vt, aa=aa, kb=kb, bO=bO, first=first, last=last: e.matmul(
                    psum[bO][:], lhsT=vt[:, kb, :], rhs=aa[:], start=first, stop=last, skip_group_check=True),
                     reads=[vb_, ab], writes=[psb[bO]])

            NT_ = len(tiles)
            SK = 2
            for n in range(min(SK, NT_)):
                stageA(n)
            for step in range(NT_ + 2):
                if step + SK < NT_:
                    stageA(step + SK)
                if 0 <= step - 1 < NT_:
                    stageB2(step - 1)
                if step < NT_:
                    stageB1(step)
                if 0 <= step - 2 < NT_:
                    stageB3(step - 2)
            for gi_, g in enumerate(gl):
                bO = 4 + gi_
                ao, aob, aos = ao_r.next()
                copy_op("act" if gi_ == 0 else "dve", ao[:], psum[bO][:], [psb[bO]], [aob])
                P.dma("act", atTs[h, :, g * 512:(g + 1) * 512], ao[:], aos, reads=[aob], writes=[dbuf["atTs"]])
    end_phase()

    def pre_mix():
        make_stage("sgl", [128, 512], BF16, 3)
        make_stage("yst", [128, 512], F32, 3)
        make_stage("mst", [128, 512], BF16, 3)

    def epi_ya(gi, t0, T, pl):
        for nb, b in enumerate(pl):
            fb = gi * 4 + nb
            sg, sgb, sgs = st_holder["sgl"].next()
            P.dma("sp", sg[:, 0:T], sgTs[fb, :, 128 + t0:128 + t0 + T], sgs, reads=[dbuf["sgTs"]], writes=[sgb])
            ys, ysb, yss = st_holder["yst"].next()
            P.op("dve", lambda e, ys=ys, sg=sg, b=b, T=T: e.tensor_tensor(out=ys[:, 0:T], in0=psum[b][:, 0:T], in1=sg[:, 0:T],
                                                                          op=ALU.mult), reads=[psb[b], sgb], writes=[ysb])
            P.dma("act", yaTs[fb, :, t0:t0 + T], ys[:, 0:T], yss, reads=[ysb], writes=[dbuf["yaTs"]])

    gemm(atTs, "atTs", 0, H, ranges(0, TO), wgroups(wa_bf, 0, D, H), "fm", epi_ya, "wa_bf", pre=pre_mix)

    def epi_yb(gi, t0, T, pl):
        for nb, b in enumerate(pl):
            fb = gi * 4 + nb
            sg, sgb, sgs = st_holder["sgl"].next()
            P.dma("sp", sg[:, 0:T], sgTs[KC + fb, :, 128 + t0:128 + t0 + T], sgs, reads=[dbuf["sgTs"]], writes=[sgb])
            ys, ysb, yss = st_holder["yst"].next()
            P.dma("sp", ys[:, 0:T], yaTs[fb, :, t0:t0 + T], yss, reads=[dbuf["yaTs"]], writes=[ysb])
            tm, tmb, _ = st_holder["tmpm"].next()
            P.op("dve", lambda e, tm=tm, sg=sg, b=b, T=T: e.tensor_tensor(out=tm[:, 0:T], in0=psum[b][:, 0:T], in1=sg[:, 0:T],
                                                                          op=ALU.mult), reads=[psb[b], sgb], writes=[tmb])
            ms, msb, mss = st_holder["mst"].next()
            P.op("pool", lambda e, ms=ms, tm=tm, ys=ys, T=T: e.tensor_tensor(out=ms[:, 0:T], in0=tm[:, 0:T], in1=ys[:, 0:T],
                                                                             op=ALU.add), reads=[tmb, ysb], writes=[msb])
            P.dma("act", mxTs[fb, :, t0:t0 + T], ms[:, 0:T], mss, reads=[msb], writes=[dbuf["mxTs"]])

    def pre_mix2():
        pre_mix()
        st_holder["tmpm"] = Ring(P, A, "tmpm", [128, 512], F32, 2)

    gemm(cvTs, "cvTs", 0, CB, ranges(0, TO), wgroups(wb_bf, 0, D, CB), "fm", epi_yb, "wb_bf", pre=pre_mix2)

    def make_pre_res(gidx):
        def pre():
            make_stage("xres", [128, 512], F32, 3)
            make_stage("pres", [128, 512], F32, 3)
            g_bc = A.alloc("gbc", [128, D], F32)
            st_holder["gbc"] = g_bc
            st_holder["gbcb"] = Buf("gbc")
            P.dma("sp", g_bc[:], gvec[gidx:gidx + 1, :].partition_broadcast(128), S0, reads=[dbuf["gvec"]],
                  writes=[st_holder["gbcb"]])
        return pre

    def make_epi_res(xsrc, xrow0, xkey, dst, dkey, extra=None):
        def epi(gi, r0, ncols, b):
            c0 = gi * 512
            xs_, xsb, xss = st_holder["xres"].next()
            P.dma("sp", xs_[:, 0:ncols], xsrc[xrow0 + r0:xrow0 + r0 + 128, c0:c0 + ncols], xss,
                  reads=[dbuf[xkey]] if xkey else [], writes=[xsb])
            pr, prb, prs = st_holder["pres"].next()
            g_bc, gbcb = st_holder["gbc"], st_holder["gbcb"]
            if extra is None:
                P.op("dve", lambda e, pr=pr, b=b, c0=c0, ncols=ncols: e.tensor_tensor(
                    out=pr[:, 0:ncols], in0=psum[b][:, 0:ncols], in1=g_bc[:, c0:c0 + ncols], op=ALU.mult),
                     reads=[psb[b], gbcb], writes=[prb])
            else:
                extra(pr, prb, b, r0, c0, ncols, g_bc, gbcb)
            P.op("pool", lambda e, xs_=xs_, ncols=ncols: e.tensor_scalar(
                out=xs_[:, 0:ncols], in0=xs_[:, 0:ncols], scalar1=cfg.ALPHA, scalar2=None, op0=ALU.mult),
                 reads=[xsb], writes=[xsb])
            P.op("dve", lambda e, pr=pr, xs_=xs_, ncols=ncols: e.tensor_tensor(
                out=pr[:, 0:ncols], in0=pr[:, 0:ncols], in1=xs_[:, 0:ncols], op=ALU.add), reads=[prb, xsb], writes=[prb])
            P.dma("act", dst[r0:r0 + 128, c0:c0 + ncols], pr[:, 0:ncols], prs, reads=[prb], writes=[dbuf[dkey]])
        return epi

    gemm(mxTs, "mxTs", 0, KC, ranges(0, TO), wgroups(wo_bf, 0, D, KC), "tm",
         make_epi_res(xe, 128, None, pre1, "pre1"), "wo_bf", pre=make_pre_res(0))

    ln_phase(pre1, TO, "affine", gb=(ln1_g, ln1_b), dst=x1s, dst_key="x1s", src_key="pre1")
    ln_phase(x1s, TO, "modT", sc_m=4, sh_m=3, dstT=h2Ts, dstT_key="h2Ts", src_key="x1s")

    def epi_pq(gi, t0, T, pl):
        for nb, b in enumerate(pl):
            stg, sbf, ssem = st_holder["stgf"].next()
            copy_op(act_copy_alt(nb), stg[:, 0:T], psum[b][:, 0:T], [psb[b]], [sbf])
            P.dma("act", pqTs[gi * 4 + nb, :, t0:t0 + T], stg[:, 0:T], ssem, reads=[sbf], writes=[dbuf["pqTs"]])

    gemm(h2Ts, "h2Ts", 0, KC, ranges(0, TO), wgroups(wq_bf, 0, PH * 256, KC), "fm", epi_pq, "wq_bf",
         pre=lambda: make_stage("stgf", [128, 512], F32, 4))

    A.reset()
    uld = Ring(P, A, "uld", [128, D], F32, 3, dma=True)
    ust = Ring(P, A, "ustT", [128, KC, 512], BF16, 2, dma=True)
    puT_v = puT_bf.rearrange("(kc p) e -> p kc e", p=128)
    cnt = 0
    for eb in range(NE // 512):
        us, usb, uss = ust.next()
        for j in range(4):
            ut, ubf, usem = uld.next()
            e0 = eb * 512 + j * 128
            P.dma("sp", ut[:], pu[e0:e0 + 128, :], usem, writes=[ubf])
            for k0 in range(0, KC, 4):
                nk = min(4, KC - k0)
                pb = (cnt % 4)
                cnt += 1
                for jj in range(nk):
                    kc = k0 + jj
                    P.op("pe", lambda e, ut=ut, kc=kc, jj=jj, pb=pb: e.transpose(
                        out=psum[pb][:, jj * 128:(jj + 1) * 128], in_=ut[:, kc * 128:(kc + 1) * 128], identity=ident_f[:]),
                         reads=[ubf, B_const], writes=[psb[pb]])
                eng = act_copy_alt(cnt)
                o = us[:, k0:k0 + nk, j * 128:(j + 1) * 128]
                i_ = psum[pb][:, 0:nk * 128].rearrange("p (k e) -> p k e", e=128)
                copy_op(eng, o, i_, [psb[pb]], [usb])
        P.dma("act", puT_v[:, :, eb * 512:(eb + 1) * 512], us[:], uss, reads=[usb], writes=[dbuf["puT_bf"]])
    end_phase()

    A.reset()
    IG = 4
    pql = Ring(P, A, "pql", [128, 2 * PH, 128], F32, 2, dma=True)
    NT4 = 4
    s_sb = A.alloc("s_sb", [128, NT4, 2 * PH, 128], F32)
    E_sb = A.alloc("E_sb", [128, NT4, 2 * PH, 128], F32)
    thr = A.alloc("thr", [128, NT4, PH, 4], F32)
    v16 = A.alloc("v16", [128, 2 * PH, 16], F32)
    stmp = A.alloc("stmp", [128, 256], F32)
    cand = A.alloc("cand", [128, PH, 256], F32)
    c16 = A.alloc("c16", [128, PH, 16], F32)
    junk = A.alloc("junk", [128, 16], F32)
    B_s = [Buf("s%d" % i) for i in range(NT4)]
    B_tk = Buf("tk")
    Aw = Ring(P, A, "Aw", [128, PH, IG, 128], F32, 2)
    Pw = Ring(P, A, "Pw", [128, PH, IG, 128], F32, 2)
    Gw = Ring(P, A, "Gw", [128, PH, IG, 128], BF16, 2)
    gst_r = Ring(P, A, "gstg", [128, IG, 512], BF16, 2, dma=True)
    for st_i in range(NST):
        t0 = st_i * 512
        for tt in range(NT4):
            pq, pqb, pqs = pql.next()
            P.dma("sp", pq[:], pqTs.rearrange("hh p t -> p hh t")[:, :, t0 + tt * 128:t0 + (tt + 1) * 128], pqs,
                  reads=[dbuf["pqTs"]], writes=[pqb])
            for hq in range(0, 2 * PH, 4):
                pb = (hq // 4) % 2
                for j in range(4):
                    hh = hq + j
                    P.op("pe", lambda e, pq=pq, hh=hh, j=j, tt=tt, pb=pb: e.matmul(
                        psum[pb][:, j * 128:(j + 1) * 128], lhsT=pq[:, hh, :], rhs=k1T[:, hh, :],
                        start=True, stop=True), reads=[pqb, B_const], writes=[psb[pb]])
                P.op("act", lambda e, hq=hq, tt=tt, pb=pb: e.activation(
                    out=s_sb[:, tt, hq:hq + 4, :], in_=psum[pb][:].rearrange("p (j k) -> p j k", k=128), func=AF.Copy),
                     reads=[psb[pb]], writes=[B_s[tt]])
            for hh in range(2 * PH):
                P.op("dve", lambda e, hh=hh, tt=tt: e.max(out=v16[:, hh, 0:8], in_=s_sb[:, tt, hh, :]), reads=[B_s[tt]],
                     writes=[B_tk])
                P.op("dve", lambda e, hh=hh, tt=tt: e.match_replace(out=stmp[:, 0:128], in_to_replace=v16[:, hh, 0:8],
                                                                    in_values=s_sb[:, tt, hh, :], imm_value=-1e30),
                     reads=[B_s[tt], B_tk], writes=[B_tk])
                P.op("dve", lambda e, hh=hh: e.max(out=v16[:, hh, 8:16], in_=stmp[:, 0:128]), reads=[B_tk], writes=[B_tk])
            for h in range(PH):
                P.op("dve", lambda e, h=h: e.tensor_tensor(
                    out=cand[:, h, :].rearrange("p (a b) -> p a b", b=16),
                    in0=v16[:, 2 * h, :].unsqueeze(2).broadcast_to([128, 16, 16]),
                    in1=v16[:, 2 * h + 1, :].unsqueeze(1).broadcast_to([128, 16, 16]), op=ALU.add),
                     reads=[B_tk], writes=[B_tk])
                P.op("dve", lambda e, h=h: e.max(out=c16[:, h, 0:8], in_=cand[:, h, :]), reads=[B_tk], writes=[B_tk])
                P.op("dve", lambda e, h=h: e.match_replace(out=stmp[:], in_to_replace=c16[:, h, 0:8], in_values=cand[:, h, :],
                                                           imm_value=-1e30), reads=[B_tk], writes=[B_tk])
                P.op("dve", lambda e, h=h: e.max(out=c16[:, h, 8:16], in_=stmp[:]), reads=[B_tk], writes=[B_tk])
                P.op("dve", lambda e, h=h, tt=tt: e.tensor_copy(out=thr[:, tt, h, 0:1], in_=c16[:, h, 15:16]),
                     reads=[B_tk], writes=[B_tk])
                P.op("dve", lambda e, h=h, tt=tt: e.tensor_scalar(out=thr[:, tt, h, 1:2], in0=c16[:, h, 0:1], scalar1=-1.0,
                                                                  scalar2=None, op0=ALU.mult), reads=[B_tk], writes=[B_tk])
                P.op("act", lambda e, h=h, tt=tt: e.activation(out=junk[:], in_=c16[:, h, :], func=AF.Exp,
                                                               bias=thr[:, tt, h, 1:2], accum_out=thr[:, tt, h, 2:3]),
                     reads=[B_tk], writes=[B_tk])
                P.op("dve", lambda e, h=h, tt=tt: e.reciprocal(out=thr[:, tt, h, 3:4], in_=thr[:, tt, h, 2:3]),
                     reads=[B_tk], writes=[B_tk])
                P.op("act", lambda e, h=h, tt=tt: e.activation(out=E_sb[:, tt, 2 * h, :], in_=s_sb[:, tt, 2 * h, :], func=AF.Exp,
                                                               bias=thr[:, tt, h, 1:2]), reads=[B_s[tt], B_tk], writes=[B_tk])
                P.op("dve", lambda e, h=h, tt=tt: e.tensor_scalar(out=E_sb[:, tt, 2 * h, :], in0=E_sb[:, tt, 2 * h, :],
                                                                  scalar1=thr[:, tt, h, 3:4], scalar2=None, op0=ALU.mult),
                     reads=[B_tk], writes=[B_tk])
                P.op("act", lambda e, h=h, tt=tt: e.activation(out=E_sb[:, tt, 2 * h + 1, :], in_=s_sb[:, tt, 2 * h + 1, :],
                                                               func=AF.Exp), reads=[B_s[tt], B_tk], writes=[B_tk])
        for ig in range(128 // IG):
            gs, gsb, gss = gst_r.next()
            for tt in range(NT4):
                aw, awb, _ = Aw.next()
                pw, pwb, _ = Pw.next()
                gw, gwb, _ = Gw.next()
                sv = s_sb[:, tt].rearrange("p (h two) k -> p h two k", two=2)
                ev = E_sb[:, tt].rearrange("p (h two) k -> p h two k", two=2)
                P.op("pool", lambda e, aw=aw, sv=sv, ig=ig: e.tensor_tensor(
                    out=aw[:], in0=sv[:, :, 0, ig * IG:(ig + 1) * IG].unsqueeze(3).broadcast_to([128, PH, IG, 128]),
                    in1=sv[:, :, 1, :].unsqueeze(2).broadcast_to([128, PH, IG, 128]), op=ALU.add),
                     reads=[B_s[tt]], writes=[awb])
                P.op("pool", lambda e, pw=pw, ev=ev, ig=ig: e.tensor_tensor(
                    out=pw[:], in0=ev[:, :, 0, ig * IG:(ig + 1) * IG].unsqueeze(3).broadcast_to([128, PH, IG, 128]),
                    in1=ev[:, :, 1, :].unsqueeze(2).broadcast_to([128, PH, IG, 128]), op=ALU.mult),
                     reads=[B_tk], writes=[pwb])
                for h in range(PH):
                    P.op("dve", lambda e, gw=gw, aw=aw, pw=pw, h=h, tt=tt: e.scalar_tensor_tensor(
                        out=gw[:, h].rearrange("p i k -> p (i k)"), in0=aw[:, h].rearrange("p i k -> p (i k)"),
                        scalar=thr[:, tt, h, 0:1], in1=pw[:, h].rearrange("p i k -> p (i k)"), op0=ALU.is_ge, op1=ALU.mult),
                         reads=[awb, pwb, B_tk], writes=[gwb])
                pb = 4 + (tt % 2)
                for c in range(IG):
                    for h in range(PH):
                        P.op("pe", lambda e, gw=gw, c=c, h=h, pb=pb: e.matmul(
                            psum[pb][:, c * 128:(c + 1) * 128], lhsT=gw[:, h, c, :], rhs=ident_b[:], start=(h == 0),
                            stop=(h == PH - 1)), reads=[gwb, B_const], writes=[psb[pb]])
                P.op("act", lambda e, gs=gs, tt=tt, pb=pb: e.activation(
                    out=gs[:, :, tt * 128:(tt + 1) * 128], in_=psum[pb][:, 0:IG * 128].rearrange("p (c t) -> p c t", t=128), func=AF.Copy),
                     reads=[psb[pb]], writes=[gsb])
            P.dma("act", GTs[ig * IG:(ig + 1) * IG, :, t0:t0 + 512].rearrange("c p t -> p c t"), gs[:], gss, reads=[gsb],
                  writes=[dbuf["GTs"]])
    end_phase()

    puT_groups = wgroups(puT_bf, 0, NE, KC)

    def pre_act():
        make_stage("gld", [128, 512], BF16, 3)
        make_stage("gel", [128, 512], F32, 2)
        make_stage("acs", [128, 512], BF16, 3)

    def epi_act(gi, t0, T, pl):
        for nb, b in enumerate(pl):
            ec = gi * 4 + nb
            gl_, glb, gls = st_holder["gld"].next()
            P.dma("sp", gl_[:, 0:T], GTs[ec, :, t0:t0 + T], gls, reads=[dbuf["GTs"]], writes=[glb])
            ge, geb, _ = st_holder["gel"].next()
            P.op("act", lambda e, ge=ge, b=b, T=T: e.activation(out=ge[:, 0:T], in_=psum[b][:, 0:T], func=AF.Gelu),
                 reads=[psb[b]], writes=[geb])
            ac, acb, acs_ = st_holder["acs"].next()
            P.op("dve" if nb % 2 == 0 else "pool", lambda e, ac=ac, ge=ge, gl_=gl_, T=T: e.tensor_tensor(
                out=ac[:, 0:T], in0=ge[:, 0:T], in1=gl_[:, 0:T], op=ALU.mult), reads=[geb, glb], writes=[acb])
            P.dma("act", acTs[ec, :, t0:t0 + T], ac[:, 0:T], acs_, reads=[acb], writes=[dbuf["acTs"]])

    gemm(h2Ts, "h2Ts", 0, KC, ranges(0, TO), puT_groups, "fm", epi_act, "puT_bf", pre=pre_act)

    for seg in range(4):
        def epi_yf(gi, r0, ncols, b, seg=seg):
            stg, sbf, ssem = st_holder["stgf"].next()
            copy_op(act_copy_alt(gi + r0 // 128), stg[:, 0:ncols], psum[b][:, 0:ncols], [psb[b]], [sbf])
            P.dma("act", yfp[seg, r0:r0 + 128, gi * 512:gi * 512 + ncols], stg[:, 0:ncols], ssem, reads=[sbf],
                  writes=[dbuf["yfp"]])
        gemm(acTs, "acTs", seg * 32, 32, ranges(0, TO), wgroups(pv_bf, 0, D, 32, kc0=seg * 32), "tm", epi_yf, "pv_bf",
             pre=lambda: make_stage("stgf", [128, 512], F32, 4))

    A.reset()
    g_bc = A.alloc("gbc2", [128, D], F32)
    B_g2 = Buf("gbc2")
    P.dma("sp", g_bc[:], gvec[1:2, :].partition_broadcast(128), S0, reads=[dbuf["gvec"]], writes=[B_g2])
    pl_r = Ring(P, A, "pl", [128, 4, 1024], F32, 2, dma=True)
    x1_r = Ring(P, A, "x1l", [128, 1024], F32, 2, dma=True)
    o_r = Ring(P, A, "o2", [128, 1024], F32, 2, dma=True)
    for tt in range(TO // 128):
        r0 = tt * 128
        for c0 in range(0, D, 1024):
            nc_ = min(1024, D - c0)
            pt, ptb, pts = pl_r.next()
            P.dma("sp", pt[:, :, 0:nc_], yfp[:, r0:r0 + 128, c0:c0 + nc_].rearrange("s r c -> r s c"), pts,
                  reads=[dbuf["yfp"]], writes=[ptb])
            xt, xtb, xts = x1_r.next()
            P.dma("sp", xt[:, 0:nc_], x1s[r0:r0 + 128, c0:c0 + nc_], xts, reads=[dbuf["x1s"]], writes=[xtb])
            ot, otb, ots = o_r.next()
            P.op("dve", lambda e, ot=ot, pt=pt, nc_=nc_: e.tensor_tensor(out=ot[:, 0:nc_], in0=pt[:, 0, 0:nc_],
                                                                         in1=pt[:, 1, 0:nc_], op=ALU.add),
                 reads=[ptb], writes=[otb])
            P.op("pool", lambda e, pt=pt, nc_=nc_: e.tensor_tensor(out=pt[:, 2, 0:nc_], in0=pt[:, 2, 0:nc_],
                                                                   in1=pt[:, 3, 0:nc_], op=ALU.add),
                 reads=[ptb], writes=[ptb])
            P.op("dve", lambda e, ot=ot, pt=pt, nc_=nc_: e.tensor_tensor(out=ot[:, 0:nc_], in0=ot[:, 0:nc_],
                                                                         in1=pt[:, 2, 0:nc_], op=ALU.add),
                 reads=[ptb, otb], writes=[otb])
            P.op("dve", lambda e, ot=ot, c0=c0, nc_=nc_: e.tensor_tensor(out=ot[:, 0:nc_], in0=ot[:, 0:nc_],
                                                                         in1=g_bc[:, c0:c0 + nc_], op=ALU.mult),
                 reads=[otb, B_g2], writes=[otb])
            P.op("pool", lambda e, xt=xt, nc_=nc_: e.tensor_scalar(out=xt[:, 0:nc_], in0=xt[:, 0:nc_], scalar1=cfg.ALPHA,
                                                                   scalar2=None, op0=ALU.mult), reads=[xtb], writes=[xtb])
            P.op("dve", lambda e, ot=ot, xt=xt, nc_=nc_: e.tensor_tensor(out=ot[:, 0:nc_], in0=ot[:, 0:nc_],
                                                                         in1=xt[:, 0:nc_], op=ALU.add),
                 reads=[otb, xtb], writes=[otb])
            P.dma("act", pre2[r0:r0 + 128, c0:c0 + nc_], ot[:, 0:nc_], ots, reads=[otb], writes=[dbuf["pre2"]])
    end_phase()
    ln_phase(pre2, TO, "affine", gb=(ln2_g, ln2_b), dst=yout, dst_key="yout", src_key="pre2")

    return


def core_inputs(cfg, inp):
    D, S, KC, TO = cfg.D, cfg.S, cfg.KC, cfg.TO
    f = lambda a: np.ascontiguousarray(np.asarray(a, dtype=np.float32))
    x = f(inp["x"])
    shared = {
        "w_ada": f(inp["w_ada"][0]),
        "b_ada": f(inp["b_ada"][0]).reshape(6 * KC, 128),
        "w_in": f(inp["w_in"][0]),
        "conv_w": f(inp["conv_w"][0]),
        "conv_b": f(inp["conv_b"][0]).reshape(-1, 128),
        "conv_ln_g": f(inp["conv_ln_g"][0]).reshape(-1, 128),
        "conv_ln_b": f(inp["conv_ln_b"][0]).reshape(-1, 128),
        "w_a_proj": f(inp["w_a_proj"][0]),
        "w_b_proj": f(inp["w_b_proj"][0]),
        "w_o": f(inp["w_o"][0]),
        "ln1_g": f(inp["ln1_g"][0]).reshape(1, D),
        "ln1_b": f(inp["ln1_b"][0]).reshape(1, D),
        "peer_wq": f(inp["peer_wq"][0]),
        "peer_k1": f(inp["peer_k1"][0]),
        "peer_k2": f(inp["peer_k2"][0]),
        "peer_u": f(inp["peer_u"][0]),
        "peer_v": f(inp["peer_v"][0]),
        "ln2_g": f(inp["ln2_g"][0]).reshape(1, D),
        "ln2_b": f(inp["ln2_b"][0]).reshape(1, D),
    }
    c = f(inp["c"])
    xbs = [np.ascontiguousarray(x[b]) for b in range(x.shape[0])]
    maps = []
    for core in range(2 * cfg.CPB):
        b, j = core // cfg.CPB, core % cfg.CPB
        start = j * TO
        xe = np.zeros((TO + 128, D), np.float32)
        xe[128:] = x[b, start:start + TO]
        if j > 0:
            xe[:128] = x[b, start - 128:start]
        meta = np.zeros((128, 4), np.float32)
        meta[:, 0] = float(start)
        meta[:, 1] = 1.0 if j > 0 else 0.0
        m = dict(shared)
        m["xb"] = xbs[b]
        m["xe"] = xe
        m["cvec"] = np.ascontiguousarray(c[b].reshape(KC, 128))
        m["meta"] = meta
        maps.append(m)
    return maps


_CACHE = {}


def kernel(**inputs):
    cfg = Cfg()
    if "nc" not in _CACHE:
        _CACHE["nc"] = build(cfg)
    nc = _CACHE["nc"]
    maps = core_inputs(cfg, inputs)
    res = run_bass_kernel_spmd(nc, maps, core_ids=list(range(8)))
    out = np.zeros((2, cfg.S, cfg.D), np.float32)
    for core in range(8):
        b, j = core // cfg.CPB, core % cfg.CPB
        out[b, j * cfg.TO:(j + 1) * cfg.TO] = res.results[core]["y"]
    return out
```

```python
import numpy as np
import concourse.bass as bass
import concourse.mybir as mybir
from concourse.bass_utils import run_bass_kernel_spmd

F32 = mybir.dt.float32
BF16 = mybir.dt.bfloat16
AF = mybir.ActivationFunctionType
ALU = mybir.AluOpType
AX = mybir.AxisListType

NEG_BIG = -1.0e5


class Cfg:
    def __init__(s, D=4096, S=8192, H=16, CW=2048, PH=8, depth=1):
        s.D, s.S, s.H, s.CW, s.PH = D, S, H, CW, PH
        s.KC = D // 128
        s.SBW = H * 128
        s.CPB = 4
        s.TO = S // s.CPB
        s.NK = 128
        s.NE = 128 * 128
        s.PQ = 256
        s.CK = 31
        s.IN_COLS = 3 * s.SBW + 2 * CW + 2 * D
        s.ALPHA = float((2 * depth) ** 0.25)
        s.EPS = 1e-5
        s.QO, s.KO, s.VO = 0, s.SBW, 2 * s.SBW
        s.GAO = 3 * s.SBW
        s.GGO = 3 * s.SBW + CW
        s.TAO = 3 * s.SBW + 2 * CW
        s.TBO = s.TAO + D


class Buf:
    __slots__ = ("name", "w", "r")

    def __init__(s, name=""):
        s.name = name
        s.w = {}
        s.r = {}


class Op:
    __slots__ = ("eng", "fn", "deps", "signal", "cnt", "dsem", "dcnt")


class DSem:
    def __init__(s, h):
        s.h = h
        s.count = 0


class Prog:
    ENGS = ("pe", "act", "dve", "pool", "sp")

    def __init__(s, nc):
        s.nc = nc
        s.ops = {e: [] for e in s.ENGS}
        s.esem = {e: nc.alloc_semaphore("es_" + e) for e in s.ENGS}
        s.dsems = []
        s.free_d = []
        s.phase_d = []
        s.pending = {e: [] for e in s.ENGS}

    def dma_sem(s, persistent=False):
        if s.free_d and not persistent:
            d = s.free_d.pop()
        else:
            d = DSem(s.nc.alloc_semaphore("ds%d" % len(s.dsems)))
            s.dsems.append(d)
        if not persistent:
            s.phase_d.append(d)
        return d

    def _collect(s, eng, reads, writes):
        deps = []
        for b in reads:
            for k, t in b.w.items():
                deps.append((t, "raw"))
        for b in writes:
            for k, t in b.w.items():
                deps.append((t, "waw"))
            for k, t in b.r.items():
                deps.append((t, "war"))
        if s.pending[eng]:
            deps += [(t, "raw") for t in s.pending[eng]]
            s.pending[eng] = []
        return deps

    def op(s, eng, fn, reads=(), writes=()):
        o = Op()
        o.eng, o.fn, o.signal, o.cnt, o.dsem, o.dcnt = eng, fn, False, 0, None, 0
        o.deps = s._collect(eng, reads, writes)
        tok = ("op", o)
        for b in reads:
            b.r[eng] = tok
        for b in writes:
            b.w = {eng: tok}
            b.r = {}
        s.ops[eng].append(o)
        return o

    def dma(s, eng, out, in_, sem, reads=(), writes=(), **kw):
        o = Op()
        o.eng, o.signal, o.cnt = eng, False, 0
        o.fn = lambda e: e.dma_start(out=out, in_=in_, **kw)
        o.deps = s._collect(eng, reads, writes)
        sem.count += 16
        o.dsem, o.dcnt = sem, sem.count
        tok = ("dma", sem, sem.count)
        key = ("d", id(sem))
        for b in reads:
            b.r[key] = tok
        for b in writes:
            b.w = {key: tok}
            b.r = {}
        s.ops[eng].append(o)
        return tok

    def barrier(s):
        toks = []
        for e in s.ENGS:
            if e == "sp":
                continue
            for o in reversed(s.ops[e]):
                if o.dsem is None:
                    toks.append(("op", o))
                    break
        for d in s.dsems:
            if d.count:
                toks.append(("dma", d, d.count))
        for e in s.ENGS:
            s.pending[e] = list(toks)
        s.free_d += s.phase_d
        s.phase_d = []

    def emit(s):
        nc = s.nc
        for e in s.ENGS:
            for o in s.ops[e]:
                for t, kind in o.deps:
                    if t[0] == "op":
                        p = t[1]
                        if p.eng != e:
                            p.signal = True
                        elif kind == "raw" and e != "pe":
                            p.signal = True
        for e in s.ENGS:
            c = 0
            for o in s.ops[e]:
                if o.signal:
                    c += 1
                    o.cnt = c
        engobj = {"pe": "tensor", "act": "scalar", "dve": "vector", "pool": "gpsimd", "sp": "sync"}

        def run(e, eng):
            seen = {}
            for o in s.ops[e]:
                need = {}
                for t, kind in o.deps:
                    if t[0] == "op":
                        p = t[1]
                        if p.eng == e and not (kind == "raw" and e != "pe"):
                            continue
                        if not p.signal:
                            continue
                        sh, c = s.esem[p.eng], p.cnt
                    else:
                        sh, c = t[1].h, t[2]
                    k = id(sh)
                    if seen.get(k, 0) >= c:
                        continue
                    if k not in need or need[k][1] < c:
                        need[k] = (sh, c)
                for k, (sh, c) in need.items():
                    eng.wait_ge(sh, c)
                    seen[k] = c
                ins = o.fn(eng)
                if o.dsem is not None:
                    ins.then_inc(o.dsem.h, 16)
                elif o.signal:
                    ins.then_inc(s.esem[e], 1)
            if e == "sp":
                for d in s.dsems:
                    if d.count and seen.get(id(d.h), 0) < d.count:
                        eng.wait_ge(d.h, d.count)

        with nc.Block() as block:
            @block.tensor
            def _(eng):
                run("pe", eng)

            @block.scalar
            def _(eng):
                run("act", eng)

            @block.vector
            def _(eng):
                run("dve", eng)

            @block.gpsimd
            def _(eng):
                run("pool", eng)

            @block.sync
            def _(eng):
                run("sp", eng)


class Arena:
    def __init__(s, nc, base, limit):
        s.nc, s.base, s.off, s.limit, s.n = nc, base, base, limit, 0

    def reset(s):
        s.off = s.base

    def alloc(s, name, shape, dtype):
        esz = 2 if dtype == BF16 else 4
        per = 1
        for d in shape[1:]:
            per *= d
        nbytes = (per * esz + 63) // 64 * 64
        assert s.off + nbytes <= s.limit, ("SBUF overflow", name, s.off, nbytes, s.limit)
        s.n += 1
        t = s.nc.alloc_sbuf_tensor_at("%s_%d" % (name, s.n), list(shape), dtype, offset=s.off)
        s.off += nbytes
        return t


class Ring:
    def __init__(s, P, arena, name, shape, dtype, n, dma=False):
        s.t = [arena.alloc(name, shape, dtype) for _ in range(n)]
        s.b = [Buf(name) for _ in range(n)]
        s.sem = [P.dma_sem() for _ in range(n)] if dma else [None] * n
        s.i, s.n = 0, n

    def next(s):
        k = s.i % s.n
        s.i += 1
        return s.t[k], s.b[k], s.sem[k]


class _Stop(Exception):
    pass


def build(cfg, debug_outs=(), stop=None):
    nc = bass.Bass("TRN2", target_bir_lowering=False)
    P = Prog(nc)
    try:
        _build_body(nc, P, cfg, debug_outs, stop)
    except _Stop:
        pass
    P.emit()
    return nc


def _build_body(nc, P, cfg, debug_outs, stop):
    phase = [0]

    def end_phase():
        P.barrier()
        phase[0] += 1
        if stop is not None and phase[0] >= stop:
            raise _Stop()

    D, S, H, CW, PH, KC, TO, NE = cfg.D, cfg.S, cfg.H, cfg.CW, cfg.PH, cfg.KC, cfg.TO, cfg.NE
    SBW = cfg.SBW
    CB = CW // 128
    NST = TO // 512
    TE = TO + 128

    def din(name, shape, dt=F32):
        return nc.dram_tensor(name, list(shape), dt, kind="ExternalInput").ap()

    def dscr(name, shape, dt):
        kind = "ExternalOutput" if name in debug_outs else "Internal"
        return nc.dram_tensor(name, list(shape), dt, kind=kind).ap()

    xb = din("xb", [S, D])
    xe = din("xe", [TE, D])
    cvec = din("cvec", [KC, 128])
    meta = din("meta", [128, 4])
    w_ada = din("w_ada", [D, 6 * D])
    b_ada = din("b_ada", [6 * KC, 128])
    w_in = din("w_in", [D, cfg.IN_COLS])
    conv_w = din("conv_w", [cfg.CK, CW])
    conv_b = din("conv_b", [CB, 128])
    conv_g = din("conv_ln_g", [CB, 128])
    conv_bb = din("conv_ln_b", [CB, 128])
    w_a = din("w_a_proj", [SBW, D])
    w_b = din("w_b_proj", [CW, D])
    w_o = din("w_o", [D, D])
    ln1_g = din("ln1_g", [1, D])
    ln1_b = din("ln1_b", [1, D])
    wq = din("peer_wq", [D, PH * 256])
    pk1 = din("peer_k1", [PH, 128, 128])
    pk2 = din("peer_k2", [PH, 128, 128])
    pu = din("peer_u", [NE, D])
    pv = din("peer_v", [NE, D])
    ln2_g = din("ln2_g", [1, D])
    ln2_b = din("ln2_b", [1, D])
    yout = nc.dram_tensor("y", [TO, D], F32, kind="ExternalOutput").ap()

    win_bf = dscr("win_bf", [D, cfg.IN_COLS], BF16)
    wa_bf = dscr("wa_bf", [SBW, D], BF16)
    wb_bf = dscr("wb_bf", [CW, D], BF16)
    wo_bf = dscr("wo_bf", [D, D], BF16)
    wq_bf = dscr("wq_bf", [D, PH * 256], BF16)
    pv_bf = dscr("pv_bf", [NE, D], BF16)
    puT_bf = dscr("puT_bf", [D, NE], BF16)
    hTb = dscr("hTb", [KC, 128, S], BF16)
    hTe = dscr("hTe", [KC, 128, TE], BF16)
    kTs = dscr("kTs", [H, 128, S], BF16)
    vvs = dscr("vvs", [S, SBW], BF16)
    qTs = dscr("qTs", [H, 128, TE], BF16)
    uTs = dscr("uTs", [CB, 128, TE], F32)
    cvTs = dscr("cvTs", [CB, 128, TO], BF16)
    atTs = dscr("atTs", [H, 128, TO], BF16)
    sgTs = dscr("sgTs", [2 * KC, 128, TE], BF16)
    yaTs = dscr("yaTs", [KC, 128, TO], F32)
    mxTs = dscr("mxTs", [KC, 128, TO], BF16)
    pre1 = dscr("pre1", [TO, D], F32)
    x1s = dscr("x1s", [TO, D], F32)
    h2Ts = dscr("h2Ts", [KC, 128, TO], BF16)
    pqTs = dscr("pqTs", [2 * PH, 128, TO], F32)
    GTs = dscr("GTs", [128, 128, TO], BF16)
    acTs = dscr("acTs", [128, 128, TO], BF16)
    yfp = dscr("yfp", [4, TO, D], F32)
    pre2 = dscr("pre2", [TO, D], F32)
    gvec = dscr("gvec", [2, D], F32)
    modrow_d = dscr("modrow", [1, 6 * D], F32)

    dbuf = {}
    for nm in ["win_bf", "wa_bf", "wb_bf", "wo_bf", "wq_bf", "pv_bf", "puT_bf", "hTb", "hTe", "kTs", "vvs", "qTs",
               "uTs", "cvTs", "atTs", "sgTs", "yaTs", "mxTs", "pre1", "x1s", "h2Ts", "pqTs", "GTs", "acTs", "yfp",
               "pre2", "gvec", "yout", "modrow"]:
        dbuf[nm] = Buf(nm)

    SB_BASE = 16640
    SB_LIMIT = 229344 - 64
    pers = Arena(nc, SB_BASE, SB_BASE + 28 * 1024)
    ident_f = pers.alloc("ident_f", [128, 128], F32)
    ident_b = pers.alloc("ident_b", [128, 128], BF16)
    ones_f = pers.alloc("ones_f", [128, 128], F32)
    ntri_b = pers.alloc("ntri_b", [128, 128], BF16)
    ntrs_b = pers.alloc("ntrs_b", [128, 128], BF16)
    tmp_f = pers.alloc("tmp_f", [128, 128], F32)
    metat = pers.alloc("metat", [128, 4], F32)
    modT = pers.alloc("modT", [128, 6 * KC], F32)
    cbT = pers.alloc("cbT", [128, CB], F32)
    cgT = pers.alloc("cgT", [128, CB], F32)
    cbbT = pers.alloc("cbbT", [128, CB], F32)
    cwT = pers.alloc("cwT", [128, CB, cfg.CK], F32)
    k1T = pers.alloc("k1T", [128, 2 * PH, 128], F32)
    B_const = Buf("const")
    B_mod = Buf("modT")
    A = Arena(nc, pers.off, SB_LIMIT)

    psum = [nc.alloc_psum_tensor("ps%d" % i, [128, 512], F32) for i in range(8)]
    psb = [Buf("ps%d" % i) for i in range(8)]

    S0 = P.dma_sem(persistent=True)

    def act_copy_alt(i):
        return "act" if i % 2 == 0 else "dve"

    def copy_op(eng, out, in_, reads, writes):
        if eng == "act":
            return P.op("act", lambda e: e.activation(out=out, in_=in_, func=AF.Copy), reads, writes)
        if eng == "dve":
            return P.op("dve", lambda e: e.tensor_copy(out=out, in_=in_), reads, writes)
        return P.op("pool", lambda e: e.tensor_copy(out=out, in_=in_), reads, writes)

    P.op("pool", lambda e: e.memset(ones_f[:], 1.0), writes=[B_const])
    P.op("pool", lambda e: e.affine_select(out=ident_f[:], in_=ones_f[:], pattern=[[-1, 128]], compare_op=ALU.is_equal,
                                           fill=0.0, base=0, channel_multiplier=1), reads=[B_const], writes=[B_const])
    P.op("pool", lambda e: e.tensor_copy(out=ident_b[:], in_=ident_f[:]), reads=[B_const], writes=[B_const])
    P.op("pool", lambda e: e.memset(tmp_f[:], -1.0), writes=[B_const])
    P.op("pool", lambda e: e.affine_select(out=ntri_b[:], in_=tmp_f[:], pattern=[[-1, 128]], compare_op=ALU.is_ge,
                                           fill=0.0, base=0, channel_multiplier=1), reads=[B_const], writes=[B_const])
    P.op("pool", lambda e: e.affine_select(out=ntrs_b[:], in_=tmp_f[:], pattern=[[1, 128]], compare_op=ALU.is_gt,
                                           fill=0.0, base=0, channel_multiplier=-1), reads=[B_const], writes=[B_const])
    P.dma("sp", metat[:], meta, S0, writes=[B_const])

    def convert(src, dst, rows, cols, key):
        sem = P.dma_sem(persistent=True)
        rstep = max(1, (2 * 1024 * 1024) // cols)
        r = 0
        while r < rows:
            r1 = min(rows, r + rstep)
            P.dma("pool", dst[r:r1, :], src[r:r1, :], sem, writes=[dbuf[key]], max_dma_last_dim=4096)
            r = r1

    convert(w_in, win_bf, D, cfg.IN_COLS, "win_bf")

    def load_fm_table(src2d, nrows, dst_tab, dst_buf, col0=0):
        r = 0
        while r < nrows:
            n = min(128, nrows - r)
            st, sb, ss = misc_ld.next()
            P.dma("sp", st[0:n, :], src2d[r:r + n, :], ss, writes=[sb])
            P.op("pe", lambda e, st=st, n=n: e.transpose(out=psum[7][:, 0:n], in_=st[0:n, :], identity=ident_f[0:n, 0:n]),
                 reads=[sb, B_const], writes=[psb[7]])
            P.op("dve", lambda e, r=r, n=n: e.tensor_copy(out=dst_tab[:, col0 + r:col0 + r + n], in_=psum[7][:, 0:n]),
                 reads=[psb[7]], writes=[dst_buf])
            r += n

    A.reset()
    misc_ld = Ring(P, A, "miscld", [128, 128], F32, 2, dma=True)
    cT = A.alloc("cT", [128, KC], F32)
    bT = A.alloc("bT", [128, 6 * KC], F32)
    B_cT, B_bT = Buf("cT"), Buf("bT")
    st, sb, ss = misc_ld.next()
    P.dma("sp", st[0:KC, :], cvec, ss, writes=[sb])
    P.op("act", lambda e, st=st: e.activation(out=st[0:KC, :], in_=st[0:KC, :], func=AF.Silu), reads=[sb], writes=[sb])
    P.op("pe", lambda e, st=st: e.transpose(out=psum[7][:, 0:KC], in_=st[0:KC, :], identity=ident_f[0:KC, 0:KC]),
         reads=[sb, B_const], writes=[psb[7]])
    P.op("dve", lambda e: e.tensor_copy(out=cT[:], in_=psum[7][:, 0:KC]), reads=[psb[7]], writes=[B_cT])
    load_fm_table(b_ada, 6 * KC, bT, B_bT)
    load_fm_table(conv_b, CB, cbT, B_const)
    load_fm_table(conv_g, CB, cgT, B_const)
    load_fm_table(conv_bb, CB, cbbT, B_const)
    cwst = A.alloc("cwst", [cfg.CK, CW], F32)
    B_cwst = Buf("cwst")
    P.dma("sp", cwst[:], conv_w, P.dma_sem(), writes=[B_cwst])
    for cb in range(CB):
        P.op("pe", lambda e, cb=cb: e.transpose(out=psum[7][:, 0:cfg.CK], in_=cwst[:, cb * 128:(cb + 1) * 128],
                                                identity=ident_f[0:cfg.CK, 0:cfg.CK]),
             reads=[B_cwst, B_const], writes=[psb[7]])
        P.op("dve", lambda e, cb=cb: e.tensor_copy(out=cwT[:, cb, :], in_=psum[7][:, 0:cfg.CK]),
             reads=[psb[7]], writes=[B_const])
    for hh in range(2 * PH):
        src = (pk1 if hh % 2 == 0 else pk2)[hh // 2]
        st, sb, ss = misc_ld.next()
        P.dma("sp", st[:], src, ss, writes=[sb])
        P.op("pe", lambda e, st=st: e.transpose(out=psum[7][:, 0:128], in_=st[:], identity=ident_f[:]),
             reads=[sb, B_const], writes=[psb[7]])
        P.op("dve", lambda e, hh=hh: e.tensor_copy(out=k1T[:, hh, :], in_=psum[7][:, 0:128]),
             reads=[psb[7]], writes=[B_const])

    CG = 256
    wad = Ring(P, A, "wad", [128, KC, CG], F32, 2, dma=True)
    rst = Ring(P, A, "rowst", [1, CG], F32, 4, dma=True)
    modraw = A.alloc("modraw", [128, 6 * KC], F32)
    B_mraw = Buf("modraw")
    w_ada_v = w_ada.rearrange("(kc p) n -> p kc n", p=128)
    for g in range(6 * D // CG):
        wt, wbuf, wsem = wad.next()
        P.dma("sp", wt[:], w_ada_v[:, :, g * CG:(g + 1) * CG], wsem, writes=[wbuf])
        b = 4 + g % 2
        for kc in range(KC):
            P.op("pe", lambda e, wt=wt, kc=kc, b=b: e.matmul(
                psum[b][0:1, 0:CG], lhsT=cT[:, kc:kc + 1], rhs=wt[:, kc, :],
                start=(kc == 0), stop=(kc == KC - 1)), reads=[wbuf, B_cT], writes=[psb[b]])
        rs, rsb, rss = rst.next()
        copy_op(act_copy_alt(g), rs[0:1, :], psum[b][0:1, 0:CG], [psb[b]], [rsb])
        P.dma("act", modrow_d[0:1, g * CG:(g + 1) * CG], rs[0:1, :], rss, reads=[rsb], writes=[dbuf["modrow"]])
    end_phase()
    convert(w_a, wa_bf, SBW, D, "wa_bf")
    convert(w_b, wb_bf, CW, D, "wb_bf")
    convert(w_o, wo_bf, D, D, "wo_bf")
    convert(wq, wq_bf, D, PH * 256, "wq_bf")
    convert(pv, pv_bf, NE, D, "pv_bf")
    load_fm_table(modrow_d[0].rearrange("(r p) -> r p", p=128), 6 * KC, modraw, B_mraw)
    P.op("dve", lambda e: e.tensor_tensor(out=modT[:], in0=modraw[:], in1=bT[:], op=ALU.add),
         reads=[B_mraw, B_bT], writes=[B_mod])
    for m in (1, 4):
        P.op("dve", lambda e, m=m: e.tensor_scalar(out=modT[:, m * KC:(m + 1) * KC], in0=modT[:, m * KC:(m + 1) * KC],
                                                   scalar1=1.0, scalar2=None, op0=ALU.add), reads=[B_mod], writes=[B_mod])
    gst = A.alloc("gst", [KC, 2, 128], F32)
    B_gst = Buf("gst")
    for i, m in enumerate((2, 5)):
        P.op("pe", lambda e, m=m: e.transpose(out=psum[7][0:KC, 0:128], in_=modT[:, m * KC:(m + 1) * KC], identity=ident_f[:]),
             reads=[B_mod, B_const], writes=[psb[7]])
        P.op("dve", lambda e, i=i: e.tensor_copy(out=gst[:, i, :], in_=psum[7][0:KC, 0:128]), reads=[psb[7]], writes=[B_gst])
        P.dma("sp", gvec[i].rearrange("(kc p) -> kc p", p=128), gst[:, i, :], S0, reads=[B_gst], writes=[dbuf["gvec"]])
    end_phase()

    def ln_phase(src, ntok, mode, sc_m=None, sh_m=None, dstT=None, dstT_key=None, gb=None, dst=None, dst_key=None,
                 src_key=None):
        A.reset()
        xr = Ring(P, A, "lnx", [128, D], F32, 2, dma=True)
        xn_r = Ring(P, A, "lnxn", [128, D], F32, 2, dma=(mode == "affine"))
        nch = (D + 511) // 512
        stt = A.alloc("lnst", [128, nch, 6], F32)
        mv = A.alloc("lnmv", [128, 8], F32)
        B_st = Buf("lnst")
        if mode == "modT":
            hst = Ring(P, A, "hst", [128, KC, 512], BF16, 2, dma=True)
        else:
            g_bc = A.alloc("g_bc", [128, D], F32)
            b_bc = A.alloc("b_bc", [128, D], F32)
            B_gb = Buf("gb")
            P.dma("sp", g_bc[:], gb[0].partition_broadcast(128), S0, writes=[B_gb])
            P.dma("sp", b_bc[:], gb[1].partition_broadcast(128), S0, writes=[B_gb])
        t = 0
        while t < ntok:
            T = min(512, ntok - t)
            if mode == "modT":
                hs, hb, hsem = hst.next()
            for tt in range(T // 128):
                r0 = t + tt * 128
                xt, xbf, xs = xr.next()
                rd = [dbuf[src_key]] if src_key else []
                P.dma("sp", xt[:], src[r0:r0 + 128, :], xs, reads=rd, writes=[xbf])
                for c in range(nch):
                    c1 = min(D, (c + 1) * 512)
                    P.op("dve", lambda e, xt=xt, c=c, c1=c1: e.bn_stats(out=stt[:, c, :], in_=xt[:, c * 512:c1]),
                         reads=[xbf], writes=[B_st])
                P.op("dve", lambda e: e.bn_aggr(out=mv[:, 0:2], in_=stt[:].rearrange("p c s -> p (c s)")),
                     reads=[B_st], writes=[B_st])
                P.op("dve", lambda e: e.tensor_scalar(out=mv[:, 2:3], in0=mv[:, 1:2], scalar1=cfg.EPS, scalar2=None,
                                                      op0=ALU.add), reads=[B_st], writes=[B_st])
                P.op("act", lambda e: e.activation(out=mv[:, 3:4], in_=mv[:, 2:3], func=AF.Sqrt), reads=[B_st], writes=[B_st])
                P.op("dve", lambda e: e.reciprocal(out=mv[:, 4:5], in_=mv[:, 3:4]), reads=[B_st], writes=[B_st])
                P.op("dve", lambda e: e.tensor_scalar(out=mv[:, 5:6], in0=mv[:, 0:1], scalar1=mv[:, 4:5], scalar2=-1.0,
                                                      op0=ALU.mult, op1=ALU.mult), reads=[B_st], writes=[B_st])
                xn, xnb, xns = xn_r.next()
                P.op("act", lambda e, xn=xn, xt=xt: e.activation(out=xn[:], in_=xt[:], func=AF.Identity,
                                                                 bias=mv[:, 5:6], scale=mv[:, 4:5]),
                     reads=[xbf, B_st], writes=[xnb])
                if mode == "modT":
                    for k0 in range(0, KC, 4):
                        nk = min(4, KC - k0)
                        pb = 4 + (k0 // 4) % 2
                        for j in range(nk):
                            kc = k0 + j
                            P.op("pe", lambda e, xn=xn, kc=kc, j=j, pb=pb: e.transpose(
                                out=psum[pb][:, j * 128:(j + 1) * 128], in_=xn[:, kc * 128:(kc + 1) * 128],
                                identity=ident_f[:]), reads=[xnb, B_const], writes=[psb[pb]])
                        for j in range(nk):
                            kc = k0 + j
                            P.op("act", lambda e, hs=hs, kc=kc, j=j, pb=pb, tt=tt: e.activation(
                                out=hs[:, kc, tt * 128:(tt + 1) * 128], in_=psum[pb][:, j * 128:(j + 1) * 128],
                                func=AF.Identity, bias=modT[:, sh_m * KC + kc:sh_m * KC + kc + 1],
                                scale=modT[:, sc_m * KC + kc:sc_m * KC + kc + 1]),
                                 reads=[psb[pb], B_mod], writes=[hb])
                else:
                    P.op("dve", lambda e, xn=xn: e.tensor_tensor(out=xn[:], in0=xn[:], in1=g_bc[:], op=ALU.mult),
                         reads=[xnb, B_gb], writes=[xnb])
                    P.op("pool", lambda e, xn=xn: e.tensor_tensor(out=xn[:], in0=xn[:], in1=b_bc[:], op=ALU.add),
                         reads=[xnb, B_gb], writes=[xnb])
                    P.dma("act", dst[r0:r0 + 128, :], xn[:], xns, reads=[xnb], writes=[dbuf[dst_key]])
            if mode == "modT":
                P.dma("act", dstT.rearrange("kc p t -> p kc t")[:, :, t:t + T], hs[:, :, 0:T], hsem, reads=[hb],
                      writes=[dbuf[dstT_key]])
            t += T
        end_phase()

    def gemm(xT, xkey, kc0, Kc, tok_ranges, groups, mode, epi, wkey, banks=(0, 1, 2, 3, 4, 5, 6, 7), xbufs=3, wbufs=2,
             arena_reset=True, pre=None):
        if arena_reset:
            A.reset()
        xr = Ring(P, A, "gx", [128, Kc, 512], BF16, xbufs, dma=True)
        wr = Ring(P, A, "gw", [128, Kc, 512], BF16, wbufs, dma=True)
        if pre is not None:
            pre()
        xv = xT.rearrange("kc p t -> p kc t")
        bi = 0
        pairs = [tok_ranges[i:i + 2] for i in range(0, len(tok_ranges), 2)]
        for pr in pairs:
            xs_l = []
            for (t0, T) in pr:
                xt, xbf, xs = xr.next()
                P.dma("sp", xt[:, :, 0:T], xv[:, kc0:kc0 + Kc, t0:t0 + T], xs, reads=[dbuf[xkey]], writes=[xbf])
                xs_l.append((xt, xbf, t0, T))
            for gi, pieces in enumerate(groups):
                wt, wbf, ws = wr.next()
                c = 0
                for (wap, n) in pieces:
                    P.dma("sp", wt[:, :, c:c + n], wap, ws, reads=[dbuf[wkey]] if wkey else [], writes=[wbf])
                    c += n
                ncols = c
                for (xt, xbf, t0, T) in xs_l:
                    if mode == "fm":
                        pl = []
                        for nb in range(ncols // 128):
                            b = banks[bi % len(banks)]
                            bi += 1
                            for kc in range(Kc):
                                P.op("pe", lambda e, b=b, wt=wt, xt=xt, kc=kc, nb=nb, T=T: e.matmul(
                                    psum[b][:, 0:T], lhsT=wt[:, kc, nb * 128:(nb + 1) * 128], rhs=xt[:, kc, 0:T],
                                    start=(kc == 0), stop=(kc == Kc - 1)), reads=[wbf, xbf], writes=[psb[b]])
                            pl.append(b)
                        epi(gi, t0, T, pl)
                    else:
                        for tt in range(T // 128):
                            b = banks[bi % len(banks)]
                            bi += 1
                            for kc in range(Kc):
                                P.op("pe", lambda e, b=b, wt=wt, xt=xt, kc=kc, tt=tt, ncols=ncols: e.matmul(
                                    psum[b][:, 0:ncols], lhsT=xt[:, kc, tt * 128:(tt + 1) * 128], rhs=wt[:, kc, 0:ncols],
                                    start=(kc == 0), stop=(kc == Kc - 1)), reads=[wbf, xbf], writes=[psb[b]])
                            epi(gi, t0 + tt * 128, ncols, b)
        end_phase()

    def wgroups(wbf_ap, c0, c1, Kc, kc0=0, step=512):
        wv = wbf_ap.rearrange("(kc p) n -> p kc n", p=128)
        out = []
        c = c0
        while c < c1:
            n = min(step, c1 - c)
            out.append([(wv[:, kc0:kc0 + Kc, c:c + n], n)])
            c += n
        return out

    def ranges(t0, t1, step=512):
        out = []
        t = t0
        while t < t1:
            out.append((t, min(step, t1 - t)))
            t += step
        return out

    ln_phase(xb, S, "modT", sc_m=1, sh_m=0, dstT=hTb, dstT_key="hTb")

    st_holder = {}

    def make_stage(name, shape, dtype, n=3):
        st_holder[name] = Ring(P, A, name, shape, dtype, n, dma=True)

    def epi_fm_store(dst, dkey, row_of_block, func=None, tcol0=0):
        cnt = [0]

        def epi(gi, t0, T, pl):
            for nb, b in enumerate(pl):
                stg, sbf, ssem = st_holder["stg"].next()
                cnt[0] += 1
                if func is None:
                    copy_op(act_copy_alt(cnt[0]), stg[:, 0:T], psum[b][:, 0:T], [psb[b]], [sbf])
                else:
                    P.op("act", lambda e, stg=stg, b=b, T=T: e.activation(out=stg[:, 0:T], in_=psum[b][:, 0:T], func=func),
                         reads=[psb[b]], writes=[sbf])
                P.dma("act", dst[row_of_block(gi, nb), :, tcol0 + t0:tcol0 + t0 + T], stg[:, 0:T], ssem, reads=[sbf],
                      writes=[dbuf[dkey]])
        return epi

    gemm(hTb, "hTb", 0, KC, ranges(0, S), wgroups(win_bf, cfg.KO, cfg.KO + SBW, KC), "fm",
         epi_fm_store(kTs, "kTs", lambda gi, nb: gi * 4 + nb), "win_bf",
         pre=lambda: make_stage("stg", [128, 512], BF16, 4))

    def epi_v(gi, r0, ncols, b):
        stg, sbf, ssem = st_holder["stg"].next()
        copy_op(act_copy_alt(gi + r0 // 128), stg[:, 0:ncols], psum[b][:, 0:ncols], [psb[b]], [sbf])
        P.dma("act", vvs[r0:r0 + 128, gi * 512:gi * 512 + ncols], stg[:, 0:ncols], ssem, reads=[sbf], writes=[dbuf["vvs"]])

    gemm(hTb, "hTb", 0, KC, ranges(0, S), wgroups(win_bf, cfg.VO, cfg.VO + SBW, KC), "tm", epi_v, "win_bf",
         pre=lambda: make_stage("stg", [128, 512], BF16, 4))

    ln_phase(xe, TE, "modT", sc_m=1, sh_m=0, dstT=hTe, dstT_key="hTe")
    own_ranges = [(0, 128)] + ranges(128, TE)
    gemm(hTe, "hTe", 0, KC, own_ranges, wgroups(win_bf, cfg.QO, cfg.QO + SBW, KC), "fm",
         epi_fm_store(qTs, "qTs", lambda gi, nb: gi * 4 + nb), "win_bf",
         pre=lambda: make_stage("stg", [128, 512], BF16, 4))
    gemm(hTe, "hTe", 0, KC, own_ranges, wgroups(win_bf, cfg.TAO, cfg.TAO + 2 * D, KC), "fm",
         epi_fm_store(sgTs, "sgTs", lambda gi, nb: gi * 4 + nb, func=AF.Sigmoid), "win_bf",
         pre=lambda: make_stage("stg", [128, 512], BF16, 4))

    win_v = win_bf.rearrange("(kc p) n -> p kc n", p=128)
    glu_groups = []
    for cb2 in range(CB // 2):
        glu_groups.append([(win_v[:, :, cfg.GAO + cb2 * 256:cfg.GAO + (cb2 + 1) * 256], 256),
                           (win_v[:, :, cfg.GGO + cb2 * 256:cfg.GGO + (cb2 + 1) * 256], 256)])

    def pre_glu():
        make_stage("sig", [128, 512], F32, 2)
        make_stage("ust", [128, 512], F32, 3)

    def epi_glu(gi, t0, T, pl):
        for j in range(2):
            ba, bg = pl[j], pl[2 + j]
            sg, sgb, _ = st_holder["sig"].next()
            P.op("act", lambda e, sg=sg, bg=bg, T=T: e.activation(out=sg[:, 0:T], in_=psum[bg][:, 0:T], func=AF.Sigmoid),
                 reads=[psb[bg]], writes=[sgb])
            us, usb, ussem = st_holder["ust"].next()
            P.op("dve", lambda e, us=us, sg=sg, ba=ba, T=T: e.tensor_tensor(out=us[:, 0:T], in0=psum[ba][:, 0:T],
                                                                            in1=sg[:, 0:T], op=ALU.mult),
                 reads=[psb[ba], sgb], writes=[usb])
            if t0 == 0:
                P.op("dve", lambda e, us=us, T=T: e.tensor_scalar(out=us[:, 0:T], in0=us[:, 0:T], scalar1=metat[:, 1:2],
                                                                  scalar2=None, op0=ALU.mult),
                     reads=[usb, B_const], writes=[usb])
            P.dma("act", uTs[gi * 2 + j, :, t0:t0 + T], us[:, 0:T], ussem, reads=[usb], writes=[dbuf["uTs"]])

    gemm(hTe, "hTe", 0, KC, own_ranges, glu_groups, "fm", epi_glu, "win_bf", pre=pre_glu)

    A.reset()
    CKn = cfg.CK
    uin = Ring(P, A, "uin", [128, 512 + CKn - 1], F32, 3, dma=True)
    ycv = A.alloc("ycv", [128, CB, 512], F32)
    B_ycv = [Buf("ycv%d" % i) for i in range(CB)]
    ysq = Ring(P, A, "ysq", [128, 512], F32, 2)
    mean_t = A.alloc("mean_t", [128, 512], F32)
    rstd_t = A.alloc("rstd_t", [128, 512], F32)
    B_ms = Buf("ms")
    cvo = Ring(P, A, "cvo", [128, 512], BF16, 3, dma=True)
    tmpc = Ring(P, A, "tmpc", [128, 512], F32, 2)
    import os as _os
    for st_i in list(range(NST)) * int(_os.environ.get("CONVREP", "1")):
        t0 = st_i * 512
        for cb in range(CB):
            ut, ubf, usem = uin.next()
            P.dma("sp", ut[:], uTs[cb, :, 128 + t0 - (CKn - 1):128 + t0 + 512], usem, reads=[dbuf["uTs"]], writes=[ubf])
            P.op("dve", lambda e, ut=ut, cb=cb: e.tensor_scalar(out=ycv[:, cb, :], in0=ut[:, 0:512], scalar1=cwT[:, cb, 0:1],
                                                                scalar2=cbT[:, cb:cb + 1], op0=ALU.mult, op1=ALU.add),
                 reads=[ubf, B_const], writes=[B_ycv[cb]])
            for k in range(1, CKn):
                P.op("dve", lambda e, ut=ut, cb=cb, k=k: e.scalar_tensor_tensor(
                    out=ycv[:, cb, :], in0=ut[:, k:k + 512], scalar=cwT[:, cb, k:k + 1], in1=ycv[:, cb, :],
                    op0=ALU.mult, op1=ALU.add), reads=[ubf, B_const, B_ycv[cb]], writes=[B_ycv[cb]])
            yq, yqb, _ = ysq.next()
            P.op("act", lambda e, yq=yq, cb=cb: e.activation(out=yq[:], in_=ycv[:, cb, :], func=AF.Square),
                 reads=[B_ycv[cb]], writes=[yqb])
            P.op("pe", lambda e, cb=cb: e.matmul(psum[0][:], lhsT=ones_f[:], rhs=ycv[:, cb, :], start=(cb == 0),
                                                 stop=(cb == CB - 1)), reads=[B_ycv[cb], B_const], writes=[psb[0]])
            P.op("pe", lambda e, cb=cb, yq=yq: e.matmul(psum[1][:], lhsT=ones_f[:], rhs=yq[:], start=(cb == 0),
                                                        stop=(cb == CB - 1)), reads=[yqb, B_const], writes=[psb[1]])
        P.op("dve", lambda e: e.tensor_scalar(out=mean_t[:], in0=psum[0][:], scalar1=1.0 / CW, scalar2=None, op0=ALU.mult),
             reads=[psb[0]], writes=[B_ms])
        P.op("dve", lambda e: e.tensor_tensor(out=rstd_t[:], in0=mean_t[:], in1=mean_t[:], op=ALU.mult),
             reads=[B_ms], writes=[B_ms])
        P.op("dve", lambda e: e.scalar_tensor_tensor(out=rstd_t[:], in0=psum[1][:], scalar=1.0 / CW, in1=rstd_t[:],
                                                     op0=ALU.mult, op1=ALU.subtract), reads=[psb[1], B_ms], writes=[B_ms])
        P.op("dve", lambda e: e.tensor_scalar(out=rstd_t[:], in0=rstd_t[:], scalar1=cfg.EPS, scalar2=None, op0=ALU.add),
             reads=[B_ms], writes=[B_ms])
        P.op("act", lambda e: e.activation(out=rstd_t[:], in_=rstd_t[:], func=AF.Sqrt), reads=[B_ms], writes=[B_ms])
        P.op("dve", lambda e: e.reciprocal(out=rstd_t[:], in_=rstd_t[:]), reads=[B_ms], writes=[B_ms])
        if "dbgm" in debug_outs and st_i == 0:
            dbgm = nc.dram_tensor("dbgm", [2, 128, 512], F32, kind="ExternalOutput").ap()
            dbgy = nc.dram_tensor("dbgy", [CB, 128, 512], F32, kind="ExternalOutput").ap()
            P.dma("sp", dbgm[0], mean_t[:], S0, reads=[B_ms])
            P.dma("sp", dbgm[1], rstd_t[:], S0, reads=[B_ms])
            for cb in range(CB):
                P.dma("sp", dbgy[cb], ycv[:, cb, :], S0, reads=[B_ycv[cb]])
        for cb in range(CB):
            tc_, tcb, _ = tmpc.next()
            P.op("dve", lambda e, tc_=tc_, cb=cb: e.tensor_tensor(out=tc_[:], in0=ycv[:, cb, :], in1=mean_t[:], op=ALU.subtract),
                 reads=[B_ycv[cb], B_ms], writes=[tcb])
            P.op("pool", lambda e, tc_=tc_: e.tensor_tensor(out=tc_[:], in0=tc_[:], in1=rstd_t[:], op=ALU.mult),
                 reads=[tcb, B_ms], writes=[tcb])
            co, cob, cosem = cvo.next()
            P.op("act", lambda e, co=co, tc_=tc_, cb=cb: e.activation(out=co[:], in_=tc_[:], func=AF.Silu,
                                                                      bias=cbbT[:, cb:cb + 1], scale=cgT[:, cb:cb + 1]),
                 reads=[tcb, B_const], writes=[cob])
            P.dma("act", cvTs[cb, :, t0:t0 + 512], co[:], cosem, reads=[cob], writes=[dbuf["cvTs"]])
    end_phase()

    A.reset()
    NKB = S // 128
    NREL = NKB + TO // 128 - 1 + 3
    masktab = A.alloc("masktab", [128, NREL * 128], BF16)
    mtmp = A.alloc("mtmp", [128, NREL * 128], F32)
    B_mask = Buf("mask")
    P.op("pool", lambda e: e.iota(mtmp[:].rearrange("p (r t) -> p r t", t=128), pattern=[[-128, NREL], [-1, 128]],
                                  base=(NKB - 1) * 128, channel_multiplier=1, allow_small_or_imprecise_dtypes=True),
         writes=[B_mask])
    P.op("dve", lambda e: e.tensor_scalar(out=masktab[:], in0=mtmp[:], scalar1=metat[:, 0:1], scalar2=NEG_BIG,
                                          op0=ALU.is_ge, op1=ALU.mult), reads=[B_mask, B_const], writes=[B_mask])
    kth = Ring(P, A, "kth", [128, S], BF16, 2, dma=True)
    vth = Ring(P, A, "vth", [128, NKB, 128], BF16, 2, dma=True)
    qth = Ring(P, A, "qth", [128, TO], BF16, 2, dma=True)
    zm_r = Ring(P, A, "zm", [128, 512], F32, 5)
    e_r = Ring(P, A, "ee", [128, 512], F32, 2)
    L_r = Ring(P, A, "LL", [128, 512], BF16, 6)
    ar_r = Ring(P, A, "arg", [128, 512], F32, 3)
    a_r = Ring(P, A, "aa", [128, 512], BF16, 5)
    ao_r = Ring(P, A, "ao", [128, 512], BF16, 2, dma=True)
    scale = 128.0 ** -0.5
    NG = TO // 512
    for h in range(H):
        kt, kb_, ks = kth.next()
        vt, vb_, vs = vth.next()
        qt, qb_, qs = qth.next()
        P.dma("sp", kt[:], kTs[h], ks, reads=[dbuf["kTs"]], writes=[kb_])
        P.dma("sp", vt[:], vvs[:, h * 128:(h + 1) * 128].rearrange("(kb p) d -> p kb d", p=128), vs, reads=[dbuf["vvs"]],
              writes=[vb_])
        P.dma("sp", qt[:], qTs[h, :, 128:128 + TO], qs, reads=[dbuf["qTs"]], writes=[qb_])
        for gp in range(0, NG, 2):
            gl = [g for g in (gp, gp + 1) if g < NG]
            tiles = []
            for kb in range(NKB - 1, -1, -1):
                for gi_, g in enumerate(gl):
                    tiles.append((gi_, g, kb))
            st8 = {}

            def stageA(n):
                gi_, g, kb = tiles[n]
                bS = 6 + (n % 2) if False else (n % 2)
                P.op("pe", lambda e, kt=kt, qt=qt, kb=kb, g=g, bS=bS: e.matmul(
                    psum[bS][:], lhsT=kt[:, kb * 128:(kb + 1) * 128], rhs=qt[:, g * 512:(g + 1) * 512],
                    start=True, stop=True), reads=[kb_, qb_], writes=[psb[bS]])
                r0 = g * 4 - kb + (NKB - 1)
                zm, zmb, _ = zm_r.next()
                P.op("dve", lambda e, zm=zm, bS=bS, r0=r0: e.tensor_tensor(
                    out=zm[:], in0=psum[bS][:], in1=masktab[:, r0 * 128:(r0 + 4) * 128], op=ALU.add),
                     reads=[psb[bS], B_mask], writes=[zmb])
                ee, eb, _ = e_r.next()
                P.op("act", lambda e, ee=ee, zm=zm: e.activation(out=ee[:], in_=zm[:], func=AF.Exp, scale=scale),
                     reads=[zmb], writes=[eb])
                LL, Lb, _ = L_r.next()
                P.op("act", lambda e, LL=LL, ee=ee: e.activation(out=LL[:], in_=ee[:], func=AF.Ln, bias=1.0),
                     reads=[eb], writes=[Lb])
                st8[n] = dict(zm=zm, zmb=zmb, LL=LL, Lb=Lb)

            def stageB1(n):
                gi_, g, kb = tiles[n]
                d_ = st8[n]
                bC = 2 + gi_
                first = (kb == NKB - 1)
                LL, Lb, zm, zmb = d_["LL"], d_["Lb"], d_["zm"], d_["zmb"]
                P.op("pe", lambda e, LL=LL, bC=bC, first=first: e.matmul(
                    psum[bC][:], lhsT=ntri_b[:], rhs=LL[:], start=first, stop=False, skip_group_check=True),
                     reads=[Lb, B_const], writes=[psb[bC]])
                ar, arb, _ = ar_r.next()
                P.op("dve", lambda e, ar=ar, zm=zm, bC=bC: e.scalar_tensor_tensor(
                    out=ar[:], in0=zm[:], scalar=scale, in1=psum[bC][:], op0=ALU.mult, op1=ALU.add),
                     reads=[zmb, psb[bC]], writes=[arb])
                aa, ab, _ = a_r.next()
                P.op("act", lambda e, aa=aa, ar=ar: e.activation(out=aa[:], in_=ar[:], func=AF.Exp),
                     reads=[arb], writes=[ab])
                d_["aa"], d_["ab"] = aa, ab

            def stageB2(n):
                gi_, g, kb = tiles[n]
                d_ = st8[n]
                bC = 2 + gi_
                last = (kb == 0)
                LL, Lb = d_["LL"], d_["Lb"]
                P.op("pe", lambda e, LL=LL, bC=bC, last=last: e.matmul(
                    psum[bC][:], lhsT=ntrs_b[:], rhs=LL[:], start=False, stop=last, skip_group_check=True),
                     reads=[Lb, B_const], writes=[psb[bC]])

            def stageB3(n):
                gi_, g, kb = tiles[n]
                d_ = st8.pop(n)
                bO = 4 + gi_
                first, last = (kb == NKB - 1), (kb == 0)
                aa, ab = d_["aa"], d_["ab"]
                P.op("pe", lambda e, vt=vt, aa=aa, kb=kb, bO=bO, first=first, last=last: e.matmul(
                    psum[bO][:], lhsT=vt[:, kb, :], rhs=aa[:], start=first, stop=last, skip_group_check=True),
                     reads=[vb_, ab], writes=[psb[bO]])

            NT_ = len(tiles)
            SK = 2
            for n in range(min(SK, NT_)):
                stageA(n)
            for step in range(NT_ + 2):
                if step + SK < NT_:
                    stageA(step + SK)
                if 0 <= step - 1 < NT_:
                    stageB2(step - 1)
                if step < NT_:
                    stageB1(step)
                if 0 <= step - 2 < NT_:
                    stageB3(step - 2)
            for gi_, g in enumerate(gl):
                bO = 4 + gi_
                ao, aob, aos = ao_r.next()
                copy_op("act" if gi_ == 0 else "dve", ao[:], psum[bO][:], [psb[bO]], [aob])
                P.dma("act", atTs[h, :, g * 512:(g + 1) * 512], ao[:], aos, reads=[aob], writes=[dbuf["atTs"]])
    end_phase()

    def pre_mix():
        make_stage("sgl", [128, 512], BF16, 3)
        make_stage("yst", [128, 512], F32, 3)
        make_stage("mst", [128, 512], BF16, 3)

    def epi_ya(gi, t0, T, pl):
        for nb, b in enumerate(pl):
            fb = gi * 4 + nb
            sg, sgb, sgs = st_holder["sgl"].next()
            P.dma("sp", sg[:, 0:T], sgTs[fb, :, 128 + t0:128 + t0 + T], sgs, reads=[dbuf["sgTs"]], writes=[sgb])
            ys, ysb, yss = st_holder["yst"].next()
            P.op("dve", lambda e, ys=ys, sg=sg, b=b, T=T: e.tensor_tensor(out=ys[:, 0:T], in0=psum[b][:, 0:T], in1=sg[:, 0:T],
                                                                          op=ALU.mult), reads=[psb[b], sgb], writes=[ysb])
            P.dma("act", yaTs[fb, :, t0:t0 + T], ys[:, 0:T], yss, reads=[ysb], writes=[dbuf["yaTs"]])

    gemm(atTs, "atTs", 0, H, ranges(0, TO), wgroups(wa_bf, 0, D, H), "fm", epi_ya, "wa_bf", pre=pre_mix)

    def epi_yb(gi, t0, T, pl):
        for nb, b in enumerate(pl):
            fb = gi * 4 + nb
            sg, sgb, sgs = st_holder["sgl"].next()
            P.dma("sp", sg[:, 0:T], sgTs[KC + fb, :, 128 + t0:128 + t0 + T], sgs, reads=[dbuf["sgTs"]], writes=[sgb])
            ys, ysb, yss = st_holder["yst"].next()
            P.dma("sp", ys[:, 0:T], yaTs[fb, :, t0:t0 + T], yss, reads=[dbuf["yaTs"]], writes=[ysb])
            tm, tmb, _ = st_holder["tmpm"].next()
            P.op("dve", lambda e, tm=tm, sg=sg, b=b, T=T: e.tensor_tensor(out=tm[:, 0:T], in0=psum[b][:, 0:T], in1=sg[:, 0:T],
                                                                          op=ALU.mult), reads=[psb[b], sgb], writes=[tmb])
            ms, msb, mss = st_holder["mst"].next()
            P.op("pool", lambda e, ms=ms, tm=tm, ys=ys, T=T: e.tensor_tensor(out=ms[:, 0:T], in0=tm[:, 0:T], in1=ys[:, 0:T],
                                                                             op=ALU.add), reads=[tmb, ysb], writes=[msb])
            P.dma("act", mxTs[fb, :, t0:t0 + T], ms[:, 0:T], mss, reads=[msb], writes=[dbuf["mxTs"]])

    def pre_mix2():
        pre_mix()
        st_holder["tmpm"] = Ring(P, A, "tmpm", [128, 512], F32, 2)

    gemm(cvTs, "cvTs", 0, CB, ranges(0, TO), wgroups(wb_bf, 0, D, CB), "fm", epi_yb, "wb_bf", pre=pre_mix2)

    def make_pre_res(gidx):
        def pre():
            make_stage("xres", [128, 512], F32, 3)
            make_stage("pres", [128, 512], F32, 3)
            g_bc = A.alloc("gbc", [128, D], F32)
            st_holder["gbc"] = g_bc
            st_holder["gbcb"] = Buf("gbc")
            P.dma("sp", g_bc[:], gvec[gidx:gidx + 1, :].partition_broadcast(128), S0, reads=[dbuf["gvec"]],
                  writes=[st_holder["gbcb"]])
        return pre

    def make_epi_res(xsrc, xrow0, xkey, dst, dkey, extra=None):
        def epi(gi, r0, ncols, b):
            c0 = gi * 512
            xs_, xsb, xss = st_holder["xres"].next()
            P.dma("sp", xs_[:, 0:ncols], xsrc[xrow0 + r0:xrow0 + r0 + 128, c0:c0 + ncols], xss,
                  reads=[dbuf[xkey]] if xkey else [], writes=[xsb])
            pr, prb, prs = st_holder["pres"].next()
            g_bc, gbcb = st_holder["gbc"], st_holder["gbcb"]
            if extra is None:
                P.op("dve", lambda e, pr=pr, b=b, c0=c0, ncols=ncols: e.tensor_tensor(
                    out=pr[:, 0:ncols], in0=psum[b][:, 0:ncols], in1=g_bc[:, c0:c0 + ncols], op=ALU.mult),
                     reads=[psb[b], gbcb], writes=[prb])
            else:
                extra(pr, prb, b, r0, c0, ncols, g_bc, gbcb)
            P.op("pool", lambda e, xs_=xs_, ncols=ncols: e.tensor_scalar(
                out=xs_[:, 0:ncols], in0=xs_[:, 0:ncols], scalar1=cfg.ALPHA, scalar2=None, op0=ALU.mult),
                 reads=[xsb], writes=[xsb])
            P.op("dve", lambda e, pr=pr, xs_=xs_, ncols=ncols: e.tensor_tensor(
                out=pr[:, 0:ncols], in0=pr[:, 0:ncols], in1=xs_[:, 0:ncols], op=ALU.add), reads=[prb, xsb], writes=[prb])
            P.dma("act", dst[r0:r0 + 128, c0:c0 + ncols], pr[:, 0:ncols], prs, reads=[prb], writes=[dbuf[dkey]])
        return epi

    gemm(mxTs, "mxTs", 0, KC, ranges(0, TO), wgroups(wo_bf, 0, D, KC), "tm",
         make_epi_res(xe, 128, None, pre1, "pre1"), "wo_bf", pre=make_pre_res(0))

    ln_phase(pre1, TO, "affine", gb=(ln1_g, ln1_b), dst=x1s, dst_key="x1s", src_key="pre1")
    ln_phase(x1s, TO, "modT", sc_m=4, sh_m=3, dstT=h2Ts, dstT_key="h2Ts", src_key="x1s")

    def epi_pq(gi, t0, T, pl):
        for nb, b in enumerate(pl):
            stg, sbf, ssem = st_holder["stgf"].next()
            copy_op(act_copy_alt(nb), stg[:, 0:T], psum[b][:, 0:T], [psb[b]], [sbf])
            P.dma("act", pqTs[gi * 4 + nb, :, t0:t0 + T], stg[:, 0:T], ssem, reads=[sbf], writes=[dbuf["pqTs"]])

    gemm(h2Ts, "h2Ts", 0, KC, ranges(0, TO), wgroups(wq_bf, 0, PH * 256, KC), "fm", epi_pq, "wq_bf",
         pre=lambda: make_stage("stgf", [128, 512], F32, 4))

    A.reset()
    uld = Ring(P, A, "uld", [128, D], F32, 3, dma=True)
    ust = Ring(P, A, "ustT", [128, KC, 512], BF16, 2, dma=True)
    puT_v = puT_bf.rearrange("(kc p) e -> p kc e", p=128)
    cnt = 0
    for eb in range(NE // 512):
        us, usb, uss = ust.next()
        for j in range(4):
            ut, ubf, usem = uld.next()
            e0 = eb * 512 + j * 128
            P.dma("sp", ut[:], pu[e0:e0 + 128, :], usem, writes=[ubf])
            for k0 in range(0, KC, 4):
                nk = min(4, KC - k0)
                pb = (cnt % 4)
                cnt += 1
                for jj in range(nk):
                    kc = k0 + jj
                    P.op("pe", lambda e, ut=ut, kc=kc, jj=jj, pb=pb: e.transpose(
                        out=psum[pb][:, jj * 128:(jj + 1) * 128], in_=ut[:, kc * 128:(kc + 1) * 128], identity=ident_f[:]),
                         reads=[ubf, B_const], writes=[psb[pb]])
                eng = act_copy_alt(cnt)
                o = us[:, k0:k0 + nk, j * 128:(j + 1) * 128]
                i_ = psum[pb][:, 0:nk * 128].rearrange("p (k e) -> p k e", e=128)
                copy_op(eng, o, i_, [psb[pb]], [usb])
        P.dma("act", puT_v[:, :, eb * 512:(eb + 1) * 512], us[:], uss, reads=[usb], writes=[dbuf["puT_bf"]])
    end_phase()

    A.reset()
    IG = 4
    pql = Ring(P, A, "pql", [128, 2 * PH, 128], F32, 2, dma=True)
    NT4 = 4
    s_sb = A.alloc("s_sb", [128, NT4, 2 * PH, 128], F32)
    E_sb = A.alloc("E_sb", [128, NT4, 2 * PH, 128], F32)
    thr = A.alloc("thr", [128, NT4, PH, 8], F32)
    v16 = A.alloc("v16", [128, 2 * PH, 16], F32)
    stmp = A.alloc("stmp", [128, 256], F32)
    cand = A.alloc("cand", [128, PH, 256], F32)
    c16 = A.alloc("c16", [128, PH, 16], F32)
    junk = A.alloc("junk", [128, 16], F32)
    B_s = [Buf("s%d" % i) for i in range(NT4)]
    B_tk = Buf("tk")
    Aw = Ring(P, A, "Aw", [128, PH, IG, 128], F32, 2)
    Pw = Ring(P, A, "Pw", [128, PH, IG, 128], F32, 2)
    Gw = Ring(P, A, "Gw", [128, PH, IG, 128], BF16, 2)
    gst_r = Ring(P, A, "gstg", [128, IG, 512], BF16, 2, dma=True)
    for st_i in range(NST):
        t0 = st_i * 512
        for tt in range(NT4):
            pq, pqb, pqs = pql.next()
            P.dma("sp", pq[:], pqTs.rearrange("hh p t -> p hh t")[:, :, t0 + tt * 128:t0 + (tt + 1) * 128], pqs,
                  reads=[dbuf["pqTs"]], writes=[pqb])
            for hq in range(0, 2 * PH, 4):
                pb = (hq // 4) % 2
                for j in range(4):
                    hh = hq + j
                    P.op("pe", lambda e, pq=pq, hh=hh, j=j, tt=tt, pb=pb: e.matmul(
                        psum[pb][:, j * 128:(j + 1) * 128], lhsT=pq[:, hh, :], rhs=k1T[:, hh, :],
                        start=True, stop=True), reads=[pqb, B_const], writes=[psb[pb]])
                P.op("act", lambda e, hq=hq, tt=tt, pb=pb: e.activation(
                    out=s_sb[:, tt, hq:hq + 4, :], in_=psum[pb][:].rearrange("p (j k) -> p j k", k=128), func=AF.Copy),
                     reads=[psb[pb]], writes=[B_s[tt]])
            for hh in range(2 * PH):
                P.op("dve", lambda e, hh=hh, tt=tt: e.max(out=v16[:, hh, 0:8], in_=s_sb[:, tt, hh, :]), reads=[B_s[tt]],
                     writes=[B_tk])
                P.op("dve", lambda e, hh=hh, tt=tt: e.match_replace(out=stmp[:, 0:128], in_to_replace=v16[:, hh, 0:8],
                                                                    in_values=s_sb[:, tt, hh, :], imm_value=-1e30),
                     reads=[B_s[tt], B_tk], writes=[B_tk])
                P.op("dve", lambda e, hh=hh: e.max(out=v16[:, hh, 8:16], in_=stmp[:, 0:128]), reads=[B_tk], writes=[B_tk])
            for h in range(PH):
                P.op("dve", lambda e, h=h: e.tensor_tensor(
                    out=cand[:, h, :].rearrange("p (a b) -> p a b", b=16),
                    in0=v16[:, 2 * h, :].unsqueeze(2).broadcast_to([128, 16, 16]),
                    in1=v16[:, 2 * h + 1, :].unsqueeze(1).broadcast_to([128, 16, 16]), op=ALU.add),
                     reads=[B_tk], writes=[B_tk])
                P.op("dve", lambda e, h=h: e.max(out=c16[:, h, 0:8], in_=cand[:, h, :]), reads=[B_tk], writes=[B_tk])
                P.op("dve", lambda e, h=h: e.match_replace(out=stmp[:], in_to_replace=c16[:, h, 0:8], in_values=cand[:, h, :],
                                                           imm_value=-1e30), reads=[B_tk], writes=[B_tk])
                P.op("dve", lambda e, h=h: e.max(out=c16[:, h, 8:16], in_=stmp[:]), reads=[B_tk], writes=[B_tk])
                P.op("dve", lambda e, h=h, tt=tt: e.tensor_copy(out=thr[:, tt, h, 0:1], in_=c16[:, h, 15:16]),
                     reads=[B_tk], writes=[B_tk])
                P.op("dve", lambda e, h=h, tt=tt: e.tensor_scalar(out=thr[:, tt, h, 1:2], in0=c16[:, h, 0:1], scalar1=-1.0,
                                                                  scalar2=None, op0=ALU.mult), reads=[B_tk], writes=[B_tk])
                P.op("act", lambda e, h=h, tt=tt: e.activation(out=junk[:], in_=c16[:, h, :], func=AF.Exp,
                                                               bias=thr[:, tt, h, 1:2], accum_out=thr[:, tt, h, 2:3]),
                     reads=[B_tk], writes=[B_tk])
                P.op("act", lambda e, h=h, tt=tt: e.activation(out=thr[:, tt, h, 4:5], in_=thr[:, tt, h, 2:3], func=AF.Ln),
                     reads=[B_tk], writes=[B_tk])
                P.op("dve", lambda e, h=h, tt=tt: e.tensor_tensor(out=thr[:, tt, h, 5:6], in0=thr[:, tt, h, 1:2],
                                                                  in1=thr[:, tt, h, 4:5], op=ALU.subtract),
                     reads=[B_tk], writes=[B_tk])
        for ig in range(128 // IG):
            gs, gsb, gss = gst_r.next()
            for tt in range(NT4):
                aw, awb, _ = Aw.next()
                pw, pwb, _ = Pw.next()
                gw, gwb, _ = Gw.next()
                sv = s_sb[:, tt].rearrange("p (h two) k -> p h two k", two=2)
                ev = E_sb[:, tt].rearrange("p (h two) k -> p h two k", two=2)
                P.op("pool", lambda e, aw=aw, sv=sv, ig=ig: e.tensor_tensor(
                    out=aw[:], in0=sv[:, :, 0, ig * IG:(ig + 1) * IG].unsqueeze(3).broadcast_to([128, PH, IG, 128]),
                    in1=sv[:, :, 1, :].unsqueeze(2).broadcast_to([128, PH, IG, 128]), op=ALU.add),
                     reads=[B_s[tt]], writes=[awb])
                for h in range(PH):
                    P.op("act", lambda e, pw=pw, aw=aw, h=h, tt=tt: e.activation(
                        out=pw[:, h].rearrange("p i k -> p (i k)"), in_=aw[:, h].rearrange("p i k -> p (i k)"), func=AF.Exp,
                        bias=thr[:, tt, h, 5:6]), reads=[awb, B_tk], writes=[pwb])
                for h in range(PH):
                    P.op("dve", lambda e, gw=gw, aw=aw, pw=pw, h=h, tt=tt: e.scalar_tensor_tensor(
                        out=gw[:, h].rearrange("p i k -> p (i k)"), in0=aw[:, h].rearrange("p i k -> p (i k)"),
                        scalar=thr[:, tt, h, 0:1], in1=pw[:, h].rearrange("p i k -> p (i k)"), op0=ALU.is_ge, op1=ALU.mult),
                         reads=[awb, pwb, B_tk], writes=[gwb])
                pb = 4 + (tt % 2)
                for c in range(IG):
                    for h in range(PH):
                        P.op("pe", lambda e, gw=gw, c=c, h=h, pb=pb: e.matmul(
                            psum[pb][:, c * 128:(c + 1) * 128], lhsT=gw[:, h, c, :], rhs=ident_b[:], start=(h == 0),
                            stop=(h == PH - 1)), reads=[gwb, B_const], writes=[psb[pb]])
                P.op("act", lambda e, gs=gs, tt=tt, pb=pb: e.activation(
                    out=gs[:, :, tt * 128:(tt + 1) * 128], in_=psum[pb][:, 0:IG * 128].rearrange("p (c t) -> p c t", t=128), func=AF.Copy),
                     reads=[psb[pb]], writes=[gsb])
            P.dma("act", GTs[ig * IG:(ig + 1) * IG, :, t0:t0 + 512].rearrange("c p t -> p c t"), gs[:], gss, reads=[gsb],
                  writes=[dbuf["GTs"]])
    end_phase()

    puT_groups = wgroups(puT_bf, 0, NE, KC)

    def pre_act():
        make_stage("gld", [128, 512], BF16, 3)
        make_stage("gel", [128, 512], F32, 2)
        make_stage("acs", [128, 512], BF16, 3)

    def epi_act(gi, t0, T, pl):
        for nb, b in enumerate(pl):
            ec = gi * 4 + nb
            gl_, glb, gls = st_holder["gld"].next()
            P.dma("sp", gl_[:, 0:T], GTs[ec, :, t0:t0 + T], gls, reads=[dbuf["GTs"]], writes=[glb])
            ge, geb, _ = st_holder["gel"].next()
            P.op("act", lambda e, ge=ge, b=b, T=T: e.activation(out=ge[:, 0:T], in_=psum[b][:, 0:T], func=AF.Gelu),
                 reads=[psb[b]], writes=[geb])
            ac, acb, acs_ = st_holder["acs"].next()
            P.op("dve" if nb % 2 == 0 else "pool", lambda e, ac=ac, ge=ge, gl_=gl_, T=T: e.tensor_tensor(
                out=ac[:, 0:T], in0=ge[:, 0:T], in1=gl_[:, 0:T], op=ALU.mult), reads=[geb, glb], writes=[acb])
            P.dma("act", acTs[ec, :, t0:t0 + T], ac[:, 0:T], acs_, reads=[acb], writes=[dbuf["acTs"]])

    gemm(h2Ts, "h2Ts", 0, KC, ranges(0, TO), puT_groups, "fm", epi_act, "puT_bf", pre=pre_act)

    for seg in range(4):
        def epi_yf(gi, r0, ncols, b, seg=seg):
            stg, sbf, ssem = st_holder["stgf"].next()
            copy_op(act_copy_alt(gi + r0 // 128), stg[:, 0:ncols], psum[b][:, 0:ncols], [psb[b]], [sbf])
            P.dma("act", yfp[seg, r0:r0 + 128, gi * 512:gi * 512 + ncols], stg[:, 0:ncols], ssem, reads=[sbf],
                  writes=[dbuf["yfp"]])
        gemm(acTs, "acTs", seg * 32, 32, ranges(0, TO), wgroups(pv_bf, 0, D, 32, kc0=seg * 32), "tm", epi_yf, "pv_bf",
             pre=lambda: make_stage("stgf", [128, 512], F32, 4))

    A.reset()
    g_bc = A.alloc("gbc2", [128, D], F32)
    B_g2 = Buf("gbc2")
    P.dma("sp", g_bc[:], gvec[1:2, :].partition_broadcast(128), S0, reads=[dbuf["gvec"]], writes=[B_g2])
    pl_r = Ring(P, A, "pl", [128, 4, 1024], F32, 2, dma=True)
    x1_r = Ring(P, A, "x1l", [128, 1024], F32, 2, dma=True)
    o_r = Ring(P, A, "o2", [128, 1024], F32, 2, dma=True)
    for tt in range(TO // 128):
        r0 = tt * 128
        for c0 in range(0, D, 1024):
            nc_ = min(1024, D - c0)
            pt, ptb, pts = pl_r.next()
            P.dma("sp", pt[:, :, 0:nc_], yfp[:, r0:r0 + 128, c0:c0 + nc_].rearrange("s r c -> r s c"), pts,
                  reads=[dbuf["yfp"]], writes=[ptb])
            xt, xtb, xts = x1_r.next()
            P.dma("sp", xt[:, 0:nc_], x1s[r0:r0 + 128, c0:c0 + nc_], xts, reads=[dbuf["x1s"]], writes=[xtb])
            ot, otb, ots = o_r.next()
            P.op("dve", lambda e, ot=ot, pt=pt, nc_=nc_: e.tensor_tensor(out=ot[:, 0:nc_], in0=pt[:, 0, 0:nc_],
                                                                         in1=pt[:, 1, 0:nc_], op=ALU.add),
                 reads=[ptb], writes=[otb])
            P.op("pool", lambda e, pt=pt, nc_=nc_: e.tensor_tensor(out=pt[:, 2, 0:nc_], in0=pt[:, 2, 0:nc_],
                                                                   in1=pt[:, 3, 0:nc_], op=ALU.add),
                 reads=[ptb], writes=[ptb])
            P.op("dve", lambda e, ot=ot, pt=pt, nc_=nc_: e.tensor_tensor(out=ot[:, 0:nc_], in0=ot[:, 0:nc_],
                                                                         in1=pt[:, 2, 0:nc_], op=ALU.add),
                 reads=[ptb, otb], writes=[otb])
            P.op("dve", lambda e, ot=ot, c0=c0, nc_=nc_: e.tensor_tensor(out=ot[:, 0:nc_], in0=ot[:, 0:nc_],
                                                                         in1=g_bc[:, c0:c0 + nc_], op=ALU.mult),
                 reads=[otb, B_g2], writes=[otb])
            P.op("pool", lambda e, xt=xt, nc_=nc_: e.tensor_scalar(out=xt[:, 0:nc_], in0=xt[:, 0:nc_], scalar1=cfg.ALPHA,
                                                                   scalar2=None, op0=ALU.mult), reads=[xtb], writes=[xtb])
            P.op("dve", lambda e, ot=ot, xt=xt, nc_=nc_: e.tensor_tensor(out=ot[:, 0:nc_], in0=ot[:, 0:nc_],
                                                                         in1=xt[:, 0:nc_], op=ALU.add),
                 reads=[otb, xtb], writes=[otb])
            P.dma("act", pre2[r0:r0 + 128, c0:c0 + nc_], ot[:, 0:nc_], ots, reads=[otb], writes=[dbuf["pre2"]])
    end_phase()
    ln_phase(pre2, TO, "affine", gb=(ln2_g, ln2_b), dst=yout, dst_key="yout", src_key="pre2")

    return


def core_inputs(cfg, inp):
    D, S, KC, TO = cfg.D, cfg.S, cfg.KC, cfg.TO
    f = lambda a: np.ascontiguousarray(np.asarray(a, dtype=np.float32))
    x = f(inp["x"])
    shared = {
        "w_ada": f(inp["w_ada"][0]),
        "b_ada": f(inp["b_ada"][0]).reshape(6 * KC, 128),
        "w_in": f(inp["w_in"][0]),
        "conv_w": f(inp["conv_w"][0]),
        "conv_b": f(inp["conv_b"][0]).reshape(-1, 128),
        "conv_ln_g": f(inp["conv_ln_g"][0]).reshape(-1, 128),
        "conv_ln_b": f(inp["conv_ln_b"][0]).reshape(-1, 128),
        "w_a_proj": f(inp["w_a_proj"][0]),
        "w_b_proj": f(inp["w_b_proj"][0]),
        "w_o": f(inp["w_o"][0]),
        "ln1_g": f(inp["ln1_g"][0]).reshape(1, D),
        "ln1_b": f(inp["ln1_b"][0]).reshape(1, D),
        "peer_wq": f(inp["peer_wq"][0]),
        "peer_k1": f(inp["peer_k1"][0]),
        "peer_k2": f(inp["peer_k2"][0]),
        "peer_u": f(inp["peer_u"][0]),
        "peer_v": f(inp["peer_v"][0]),
        "ln2_g": f(inp["ln2_g"][0]).reshape(1, D),
        "ln2_b": f(inp["ln2_b"][0]).reshape(1, D),
    }
    c = f(inp["c"])
    xbs = [np.ascontiguousarray(x[b]) for b in range(x.shape[0])]
    maps = []
    for core in range(2 * cfg.CPB):
        b, j = core // cfg.CPB, core % cfg.CPB
        start = j * TO
        xe = np.zeros((TO + 128, D), np.float32)
        xe[128:] = x[b, start:start + TO]
        if j > 0:
            xe[:128] = x[b, start - 128:start]
        meta = np.zeros((128, 4), np.float32)
        meta[:, 0] = float(start)
        meta[:, 1] = 1.0 if j > 0 else 0.0
        m = dict(shared)
        m["xb"] = xbs[b]
        m["xe"] = xe
        m["cvec"] = np.ascontiguousarray(c[b].reshape(KC, 128))
        m["meta"] = meta
        maps.append(m)
    return maps


_CACHE = {}


def kernel(**inputs):
    cfg = Cfg()
    if "nc" not in _CACHE:
        _CACHE["nc"] = build(cfg)
    nc = _CACHE["nc"]
    maps = core_inputs(cfg, inputs)
    res = run_bass_kernel_spmd(nc, maps, core_ids=list(range(8)))
    out = np.zeros((2, cfg.S, cfg.D), np.float32)
    for core in range(8):
        b, j = core // cfg.CPB, core % cfg.CPB
        out[b, j * cfg.TO:(j + 1) * cfg.TO] = res.results[core]["y"]
    return out
```
